# Optimizing a Trainium2 kernel written in Bass

```python
import jax, jax.numpy as jnp
from jax import lax
import numpy as np

D_MODEL = 2048
BATCH = 4
SEQ = 2048
DEPTH = 4
DEC_BATCH = 32
DEC_SEQ = 8
PAST_LEN = 16384
PAGE_SIZE = 128

N_A_LAYERS = DEPTH // 2
N_B_LAYERS = DEPTH - N_A_LAYERS
POOL_WINDOWS = (2, 4, 8, 16)
N_POOL_GROUPS = len(POOL_WINDOWS)
POOL_GROUP = D_MODEL // N_POOL_GROUPS
POOL_STATE = max(POOL_WINDOWS) - 1
HEAD_DIM = 64
N_HEADS = D_MODEL // HEAD_DIM
N_KV_HEADS = N_HEADS // 8
GQA_GROUP = N_HEADS // N_KV_HEADS
WINDOW = 128
BLOCK = 128
D_FF = 4 * D_MODEL
RMS_EPS = 1e-5

kernel_name = "yoco_pool_swa_sink_decoder_step"


def _rmsnorm(x, g):
    xf = x.astype(jnp.float32)
    y = xf * lax.rsqrt(jnp.mean(xf * xf, axis=-1, keepdims=True) + RMS_EPS)
    return (y * g.astype(jnp.float32)).astype(x.dtype)


def _pool_mix(u, past, pos0, w_pool, scale):
    T = u.shape[1]
    ext = jnp.concatenate([past.astype(u.dtype), u], axis=1)
    c = jnp.cumsum(ext.astype(jnp.float32), axis=1)
    c = jnp.pad(c, ((0, 0), (1, 0), (0, 0)))
    pos = pos0 + jnp.arange(T)
    uf = u.astype(jnp.float32)
    outs = []
    for g, w in enumerate(POOL_WINDOWS):
        sl = slice(g * POOL_GROUP, (g + 1) * POOL_GROUP)
        s = c[:, POOL_STATE + 1:, sl] - c[:, POOL_STATE + 1 - w:POOL_STATE + 1 - w + T, sl]
        cnt = jnp.minimum(pos + 1, w).astype(jnp.float32)
        d = s / cnt[None, :, None] - uf[:, :, sl]
        outs.append(jnp.einsum('btc,ce->bte', d.astype(u.dtype), w_pool[g]))
    out = jnp.concatenate(outs, axis=-1) * scale
    return out.astype(u.dtype), ext[:, -POOL_STATE:]


def _banded_sink_attention(q, k, v, q_pos, k_pos, sinks):
    B, N, Q = q.shape[:3]
    qg = q.reshape(B, N, Q, N_KV_HEADS, GQA_GROUP, HEAD_DIM)
    s = jnp.einsum('bnqkgd,bnskd->bnkgqs', qg, k,
                   preferred_element_type=jnp.float32) * (HEAD_DIM ** -0.5)
    rel = q_pos[:, :, None] - k_pos[:, None, :]
    allowed = (k_pos[:, None, :] >= 0) & (rel >= 0) & (rel < WINDOW)
    s = jnp.where(allowed[None, :, None, None], s, -jnp.inf)
    sk = sinks.astype(jnp.float32).reshape(N_KV_HEADS, GQA_GROUP)[None, None, :, :, None, None]
    m = jnp.maximum(jnp.max(s, axis=-1, keepdims=True), sk)
    p = jnp.exp(s - m)
    p = p / (jnp.sum(p, axis=-1, keepdims=True) + jnp.exp(sk - m))
    o = jnp.einsum('bnkgqs,bnskd->bnqkgd', p.astype(v.dtype), v)
    return o.reshape(B, N * Q, N_HEADS * HEAD_DIM)


def _attn_prompt(hn, k_sh, v_sh, w_q, w_o, sinks):
    B, T, _ = hn.shape
    nb = T // BLOCK
    q = (hn @ w_q).reshape(B, nb, BLOCK, N_HEADS, HEAD_DIM)

    def with_prev(a):
        a = a.reshape(B, nb, BLOCK, N_KV_HEADS, HEAD_DIM)
        prev = jnp.concatenate([jnp.zeros_like(a[:, :1]), a[:, :-1]], axis=1)
        return jnp.concatenate([prev, a], axis=2)

    blk = jnp.arange(nb)[:, None]
    q_pos = blk * BLOCK + jnp.arange(BLOCK)[None, :]
    k_pos = (blk - 1) * BLOCK + jnp.arange(2 * BLOCK)[None, :]
    o = _banded_sink_attention(q, with_prev(k_sh), with_prev(v_sh), q_pos, k_pos, sinks)
    return o @ w_o


def _attn_sample(hn, k_new, v_new, k_buf, v_buf, pos0, w_q, w_o, sinks):
    B, T, _ = hn.shape
    wb = k_buf.shape[1]
    q = (hn @ w_q).reshape(B, 1, T, N_HEADS, HEAD_DIM)
    kk = jnp.concatenate([k_buf.astype(k_new.dtype), k_new], axis=1)[:, None]
    vv = jnp.concatenate([v_buf.astype(v_new.dtype), v_new], axis=1)[:, None]
    new_pos = pos0 + jnp.arange(T)
    k_pos = jnp.concatenate([pos0 - wb + jnp.arange(wb), new_pos])[None]
    o = _banded_sink_attention(q, kk, vv, new_pos[None], k_pos, sinks)
    return o @ w_o


def _sq_relu_mlp(hn, w_up, w_down):
    a = jax.nn.relu(hn @ w_up)
    return (a * a) @ w_down


def _trunk(x, pool_past, k_buf, v_buf, pos0, norm_a, w_pool, pool_scale, norm_kv, w_k, w_v,
           norm_b, w_q, w_o, sinks, norm_mlp, w_up, w_down, norm_f):
    B, T, _ = x.shape
    wb = min(WINDOW, PAST_LEN)
    h = x
    new_pool = []
    k_sh = v_sh = None
    for l in range(DEPTH):
        if l < N_A_LAYERS:
            out, st = _pool_mix(_rmsnorm(h, norm_a[l]), pool_past[l], pos0, w_pool[l], pool_scale[l])
            h = h + out
            new_pool.append(st)
        else:
            j = l - N_A_LAYERS
            hn = _rmsnorm(h, norm_b[j])
            if k_buf is None:
                h = h + _attn_prompt(hn, k_sh, v_sh, w_q[j], w_o[j], sinks[j])
            else:
                h = h + _attn_sample(hn, k_sh, v_sh, k_buf, v_buf, pos0, w_q[j], w_o[j], sinks[j])
        h = h + _sq_relu_mlp(_rmsnorm(h, norm_mlp[l]), w_up[l], w_down[l])
        if l == N_A_LAYERS - 1:
            hk = _rmsnorm(h, norm_kv)
            k_sh = (hk @ w_k).reshape(B, T, N_KV_HEADS, HEAD_DIM)
            v_sh = (hk @ w_v).reshape(B, T, N_KV_HEADS, HEAD_DIM)
    y = _rmsnorm(h, norm_f)
    if k_buf is None:
        k_win, v_win = k_sh[:, T - wb:], v_sh[:, T - wb:]
    else:
        k_win = jnp.concatenate([k_buf.astype(k_sh.dtype), k_sh], axis=1)[:, -wb:]
        v_win = jnp.concatenate([v_buf.astype(v_sh.dtype), v_sh], axis=1)[:, -wb:]
    return y, jnp.stack(new_pool), k_win, v_win


def setup_inputs(seed: int = 0) -> dict:
    key = jax.random.key(seed)
    ks = jax.random.split(key, 20)
    f32 = jnp.float32
    wb = min(WINDOW, PAST_LEN)
    nrm = lambda k, s: jax.random.normal(k, s, f32)
    gain = lambda k, s: 1.0 + 0.02 * nrm(k, s)
    return {
        "x_prompt": nrm(ks[0], (BATCH, SEQ, D_MODEL)),
        "x_sample": nrm(ks[1], (DEC_BATCH, DEC_SEQ, D_MODEL)),
        "state_pool": nrm(ks[2], (N_A_LAYERS, DEC_BATCH, POOL_STATE, D_MODEL)),
        "cache_k_win": nrm(ks[3], (DEC_BATCH, wb, N_KV_HEADS, HEAD_DIM)),
        "cache_v_win": nrm(ks[4], (DEC_BATCH, wb, N_KV_HEADS, HEAD_DIM)),
        "norm_a": gain(ks[5], (N_A_LAYERS, D_MODEL)),
        "w_pool": nrm(ks[6], (N_A_LAYERS, N_POOL_GROUPS, POOL_GROUP, POOL_GROUP)) * POOL_GROUP ** -0.5,
        "pool_scale": gain(ks[7], (N_A_LAYERS, D_MODEL)),
        "norm_kv": gain(ks[8], (D_MODEL,)),
        "w_k": nrm(ks[9], (D_MODEL, N_KV_HEADS * HEAD_DIM)) * D_MODEL ** -0.5,
        "w_v": nrm(ks[10], (D_MODEL, N_KV_HEADS * HEAD_DIM)) * D_MODEL ** -0.5,
        "norm_b": gain(ks[11], (N_B_LAYERS, D_MODEL)),
        "w_q": nrm(ks[12], (N_B_LAYERS, D_MODEL, N_HEADS * HEAD_DIM)) * D_MODEL ** -0.5,
        "w_o": nrm(ks[13], (N_B_LAYERS, N_HEADS * HEAD_DIM, D_MODEL)) * (N_HEADS * HEAD_DIM) ** -0.5,
        "sinks": 0.5 * nrm(ks[14], (N_B_LAYERS, N_HEADS)),
        "norm_mlp": gain(ks[15], (DEPTH, D_MODEL)),
        "w_up": nrm(ks[16], (DEPTH, D_MODEL, D_FF)) * D_MODEL ** -0.5,
        "w_down": nrm(ks[17], (DEPTH, D_FF, D_MODEL)) * D_FF ** -0.5,
        "norm_f": gain(ks[18], (D_MODEL,)),
    }


def reference(x_prompt, x_sample, state_pool, cache_k_win, cache_v_win, norm_a, w_pool, pool_scale,
              norm_kv, w_k, w_v, norm_b, w_q, w_o, sinks, norm_mlp, w_up, w_down, norm_f):
    zero_pool = jnp.zeros((N_A_LAYERS, x_prompt.shape[0], POOL_STATE, D_MODEL), x_prompt.dtype)
    y_prompt, pool_p, k_p, v_p = _trunk(
        x_prompt, zero_pool, None, None, 0, norm_a, w_pool, pool_scale, norm_kv, w_k, w_v,
        norm_b, w_q, w_o, sinks, norm_mlp, w_up, w_down, norm_f)
    y_sample, pool_s, k_s, v_s = _trunk(
        x_sample, state_pool, cache_k_win, cache_v_win, PAST_LEN, norm_a, w_pool, pool_scale,
        norm_kv, w_k, w_v, norm_b, w_q, w_o, sinks, norm_mlp, w_up, w_down, norm_f)
    return (y_prompt, y_sample, pool_p, pool_s, k_p, v_p, k_s, v_s)
```

```python
import numpy as np
from contextlib import ExitStack
import concourse.bass as bass
import concourse.mybir as mybir
from concourse.bass_utils import run_bass_kernel_spmd

F32 = mybir.dt.float32
BF16 = mybir.dt.bfloat16
AF = mybir.ActivationFunctionType
ALU = mybir.AluOpType

D = 2048
NCH = 16
DFF = 8192
NPR = 1024
NHALO = 160
NS = 32
NM = NPR + NS
PST = 15
EPS = 1e-5
NSLOT = 8
EXT = 15 + NHALO + NPR + 4 * 23
POOL_W = (2, 4, 8, 16)

TILES_M = [("M", 0, 384, (0, 1, 2)), ("M", 384, 384, (3, 4, 5)), ("M", 768, 288, (6, 7, 8))]
TILE_H = ("H", 0, NHALO, ("h",))


class Prog:
    ENGS = ("pe", "act", "dve", "pool", "sp")
    EPOCH = 4096
    NEP = 8
    KDMA = 12

    def __init__(self, nc, es):
        self.nc = nc
        self.ops = []
        self.res = {}
        self.emitted = 0
        self.cnt = {e: 0 for e in self.ENGS}
        self.sems = {e: [es.enter_context(nc.semaphore(f"s_{e}_{i}")) for i in range(self.NEP)]
                     for e in ("pe", "act", "dve")}
        self.sems["pool"] = [es.enter_context(nc.semaphore(f"s_pool_{i}")) for i in range(2)]
        self.sems["sp"] = [es.enter_context(nc.semaphore(f"s_sp_{i}")) for i in range(2)]
        self.dsems = {q: [es.enter_context(nc.semaphore(f"d_{q}_{i}")) for i in range(self.KDMA)]
                      for q in ("pool", "sp")}
        self.duse = {q: [0] * self.KDMA for q in ("pool", "sp")}
        self.dn = {q: 0 for q in ("pool", "sp")}
        self.waited = {e: {} for e in self.ENGS}
        self.floor = 0
        self.barrier_tokens = []
        self.need_bw = {e: False for e in self.ENGS}

    cap = None

    nobar_next = False

    def add(self, eng, fn, reads=(), writes=(), dma=False):
        if self.cap is not None:
            self.cap.append((eng, fn, list(reads), list(writes), dma))
            return None
        oid = len(self.ops)
        deps = {}
        psr = [k for k in reads if k[0] == "ps"]
        if psr:
            writes = list(writes) + [k for k in psr if k not in writes]

        def need(p, raw):
            po = self.ops[p]
            if po["dma"]:
                deps[("d", p)] = p
                return
            if p < self.floor:
                return
            if (not dma) and po["eng"] == eng and eng == "pe":
                return
            k = ("c", po["eng"])
            if deps.get(k, -1) < p:
                deps[k] = p

        for k in reads:
            r = self.res.get(k)
            if r is not None and r[0] is not None:
                need(r[0], True)
        for k in writes:
            r = self.res.get(k)
            if r is not None:
                if r[0] is not None:
                    need(r[0], False)
                for p in r[1].values():
                    need(p, False)
        ek = ("d", oid) if dma else eng
        for k in reads:
            r = self.res.setdefault(k, [None, {}])
            r[1][ek] = oid
        for k in writes:
            self.res[k] = [oid, {}]
        for p in deps.values():
            self.ops[p]["inc"] = True
        self.ops.append(dict(id=oid, eng=eng, fn=fn, dma=dma, deps=list(deps.values()), inc=False, tok=None,
                             nobar=(dma and self.nobar_next)))
        return oid

    def barrier(self):
        last = {}
        for op in self.ops:
            if not op["dma"] and op["fn"] is not None:
                last[op["eng"]] = op["id"]
        for p in last.values():
            self.ops[p]["inc"] = True
        self._pending_barrier = list(last.values())

    def _assign(self, lo, hi):
        for op in self.ops[lo:hi]:
            if op["dma"]:
                q = op["eng"]
                i = self.dn[q] % self.KDMA
                u = self.duse[q][i]
                op["tok"] = (("d", q, i), 16 * (u + 1))
                op["prev"] = (("d", q, i), 16 * u) if u > 0 else None
                self.duse[q][i] += 1
                self.dn[q] += 1
            elif op["inc"]:
                e = op["eng"]
                self.cnt[e] += 1
                ep = (self.cnt[e] - 1) // self.EPOCH
                assert ep < len(self.sems[e]), ("too many incs", e)
                op["tok"] = (("c", e, ep), (self.cnt[e] - 1) % self.EPOCH + 1)

    def _sem(self, key):
        if key[0] == "d":
            return self.dsems[key[1]][key[2]]
        return self.sems[key[1]][key[2]]

    def emit_block(self):
        lo, hi = self.emitted, len(self.ops)
        self._assign(lo, hi)
        nc = self.nc
        bw = getattr(self, "_bw_tokens", [])

        def run(ename):
            def body(engine):
                w = self.waited[ename]

                def wait(tok):
                    key, val = tok
                    if w.get(key, 0) >= val:
                        return
                    engine.wait_ge(self._sem(key), val)
                    w[key] = val

                first = True
                for op in self.ops[lo:hi]:
                    if op["eng"] != ename:
                        continue
                    if first:
                        for t in bw:
                            wait(t)
                        first = False
                    for p in op["deps"]:
                        wait(self.ops[p]["tok"])
                    if op["dma"] and op["prev"] is not None:
                        wait(op["prev"])
                    if op["fn"] is None:
                        continue
                    inst = op["fn"](engine)
                    if op["tok"] is not None:
                        inst.then_inc(self._sem(op["tok"][0]), 16 if op["dma"] else 1)
            return body

        with nc.Block() as block:
            block.tensor(run("pe"))
            block.scalar(run("act"))
            block.vector(run("dve"))
            block.gpsimd(run("pool"))
            block.sync(run("sp"))
        self.emitted = hi
        pend = getattr(self, "_pending_barrier", None)
        if pend:
            toks = [self.ops[p]["tok"] for p in pend]
            dm = {}
            for op in self.ops[lo:hi]:
                if op["dma"] and not op["nobar"]:
                    key, val = op["tok"]
                    dm[key] = max(dm.get(key, 0), val)
            self._bw_tokens = toks + list(dm.items())
            self.floor = hi
            self._pending_barrier = None
        else:
            self._bw_tokens = []


STAGE = 99


def build_nc():
    nc = bass.Bass("TRN2", target_bir_lowering=False)

    def din(name, shape):
        return nc.dram_tensor(name, list(shape), F32, kind="ExternalInput").ap()

    def dout(name, shape):
        return nc.dram_tensor(name, list(shape), F32, kind="ExternalOutput").ap()

    xh = din("xh", (NHALO, D)); xp = din("xp", (NPR, D)); xs = din("xs", (NS, D))
    spd = din("sp", (2, 60, D)); ck = din("ck", (4, 128, 256)); cv = din("cv", (4, 128, 256))
    gvd = din("gv", (128, 12, 16)); skd = din("sk", (128, 2, 16)); invcd = din("invc", (128, 4, 16))
    identd = din("ident", (128, 128))
    masknd = din("maskn", (128, 512)); mask0d = din("mask0", (128, 512))
    masksbd = din("masksb", (128, 64)); masksnd = din("masksn", (32, 64))
    w_pool = din("w_pool", (2, 4, 512, 512)); wkd = din("wkd", (D, 512)); wkv = din("wkv", (D, 512))
    w_q = din("w_q", (2, D, D)); w_o = din("w_o", (2, D, D))
    w_up = din("w_up", (4, D, DFF)); w_down = din("w_down", (4, DFF, D))

    y_p = dout("y_p", (NPR, D)); y_s = dout("y_s", (NS, D))
    pool_p = dout("pool_p", (2, PST, D)); pool_s = dout("pool_s", (2, 4, PST, D))
    k_p = dout("k_p", (128, 256)); v_p = dout("v_p", (128, 256))
    k_s = dout("k_s", (4, 128, 256)); v_s = dout("v_s", (4, 128, 256))

    with ExitStack() as es:
        E = es.enter_context
        P = Prog(nc, es)

        def sb(name, shape, dt, stack=None):
            return (stack or es).enter_context(nc.sbuf_tensor("t_" + name, list(shape), dt))

        hM = sb("hM", (128, NCH, NM), F32)
        hnM = sb("hnM", (128, NCH, NM), BF16)
        agM = sb("agM", (128, 4, NM), BF16)
        ring = sb("ring", (128, NSLOT, 2048), BF16)
        rstdM = sb("rstdM", (128, NM), F32)
        sq = sb("sq", (128, 3, 384), BF16)
        rl = sb("rl", (128, 2, 384), F32)
        stage = sb("stage", (128, 2112), F32)
        gv = sb("gv", (128, 12, 16), F32)
        ident = sb("ident", (128, 128), F32)
        onesn = sb("onesn", (128, 128), BF16)
        ones64 = sb("ones64", (128, 64), BF16)
        kvo = sb("kvo", (128, 512), F32)
        psb = [E(nc.psum_tensor(f"ps{i}", [128, 512], F32)) for i in range(8)]

        rr = {"up": 0, "down": 0, "misc": 0, "fq": 0, "fo": 0, "sq": 0, "rl": 0, "piece": 0, "evac": 0}
        BANKS = {"up": (0, 1, 2), "down": (3, 4, 5), "misc": (6, 7), "fq": (4, 5), "fo": (6, 7)}

        def bank(pool):
            b = BANKS[pool][rr[pool] % len(BANKS[pool])]
            rr[pool] += 1
            return b

        def PS(b):
            return ("ps", b)

        def tview(kind, c, col0, w, M, H):
            t = M if kind == "M" else H
            return t[:, c, col0:col0 + w]

        def tkeys(name, kind, c, blocks):
            if kind == "M":
                return [(name, c, b) for b in blocks]
            return [(name + "H", c)]

        def load_piece(src_ap, view):
            s = rr["piece"] % NSLOT
            rr["piece"] += 1
            if view == "u":
                dst = ring[:, s, :].rearrange("p (k c) -> p k c", c=512)
            else:
                dst = ring[:, s, :]
            P.add("pool", lambda g, dst=dst, src=src_ap: g.dma_start(out=dst, in_=src),
                  writes=[("ring", s)], dma=True)
            return s

        def u_pieces(mat, col0):
            K = mat.shape[0] // 128
            slots = []
            for kk in range(K // 4):
                src = mat[kk * 512:(kk + 1) * 512, col0:col0 + 512].rearrange("(k p) c -> p k c", p=128)
                slots.append(load_piece(src, "u"))
            return slots

        def d_pieces(mat, row0, n):
            return [load_piece(mat[row0 + i * 128: row0 + (i + 1) * 128, :], "d") for i in range(n)]

        def w_u(slots, k, col0, w=128):
            s = slots[k // 4]
            return ring[:, s, (k % 4) * 512 + col0:(k % 4) * 512 + col0 + w]

        def evac_engine():
            rr["evac"] += 1
            return "act" if rr["evac"] % 2 else "dve"

        def copy_op(eng, out, in_, reads, writes):
            if eng == "act":
                P.add("act", lambda a: a.activation(out=out, in_=in_, func=AF.Copy), reads=reads, writes=writes)
            else:
                P.add(eng, lambda v: v.tensor_copy(out=out, in_=in_), reads=reads, writes=writes)

        P.add("sp", lambda s: s.dma_start(out=gv[:], in_=gvd), writes=[("gv",)], dma=True)
        P.add("sp", lambda s: s.dma_start(out=ident[:], in_=identd), writes=[("ident",)], dma=True)
        P.add("dve", lambda v: v.memset(onesn[:], 1.0 / D), writes=[("onesn",)])
        P.add("dve", lambda v: v.memset(ones64[:], 1.0), writes=[("ones64",)])

        def norm_stats(tiles, hMt, hHt, rM, rH):
            for (kind, col0, w, blocks) in tiles:
                b = bank("misc")
                for c in range(NCH):
                    i = rr["sq"] % 3
                    rr["sq"] += 1
                    src = tview(kind, c, col0, w, hMt, hHt)
                    P.add("act", lambda a, i=i, src=src, w=w: a.activation(out=sq[:, i, 0:w], in_=src, func=AF.Square),
                          reads=tkeys("h", kind, c, blocks), writes=[("sq", i)])
                    P.add("pe", lambda t, b=b, i=i, w=w, c=c: t.matmul(psb[b][:, 0:w], lhsT=onesn[:], rhs=sq[:, i, 0:w],
                                                                    start=(c == 0), stop=(c == NCH - 1)),
                          reads=[("sq", i), ("onesn",)], writes=[PS(b)])
                rt = (rM if kind == "M" else rH)[:, col0:col0 + w]
                rk = tkeys("rstd", kind, 0, blocks)
                P.add("act", lambda a, b=b, rt=rt, w=w: a.activation(out=rt, in_=psb[b][:, 0:w], func=AF.Ln, bias=EPS, scale=1.0),
                      reads=[PS(b)], writes=rk)
                P.add("act", lambda a, rt=rt: a.activation(out=rt, in_=rt, func=AF.Exp, scale=-0.5), reads=rk, writes=rk)

        def norm_apply(tiles, gi, hMt, hHt, rM, rH, oM, oH, oname):
            for (kind, col0, w, blocks) in tiles:
                rt = (rM if kind == "M" else rH)[:, col0:col0 + w]
                for c in range(NCH):
                    src = tview(kind, c, col0, w, hMt, hHt)
                    dst = tview(kind, c, col0, w, oM, oH)
                    P.add("dve", lambda v, dst=dst, src=src, c=c, rt=rt: v.scalar_tensor_tensor(
                        out=dst, in0=src, scalar=gv[:, gi, c:c + 1], in1=rt, op0=ALU.mult, op1=ALU.mult),
                        reads=tkeys("h", kind, c, blocks) + tkeys("rstd", kind, 0, blocks) + [("gv",)],
                        writes=tkeys(oname, kind, c, blocks))

        def up_like(tiles, slots, nout, inM, inH, evac, pool="up"):
            for tl in tiles:
                for f in range(nout):
                    (kind, col0, w, blocks) = tl
                    b = bank(pool)
                    for k in range(NCH):
                        rhs = tview(kind, k, col0, w, inM, inH)
                        P.add("pe", lambda t, b=b, w=w, k=k, f=f, rhs=rhs: t.matmul(
                            psb[b][:, 0:w], lhsT=w_u(slots, k, f * 128), rhs=rhs, start=(k == 0), stop=(k == NCH - 1)),
                            reads=[("ring", slots[k // 4])] + tkeys("hn", kind, k, blocks), writes=[PS(b)])
                    evac(f, tl, b)

        def down_like(tiles, dslots, inM, inH, inname, hMt, hHt, pool="down", after_tile=None):
            nf = len(dslots)
            for ti_, (kind, col0, w, blocks) in enumerate(tiles):
                if after_tile is not None and ti_ > 0:
                    after_tile(ti_ - 1)
                for d in range(NCH):
                    b = bank(pool)
                    for f in range(nf):
                        rhs = tview(kind, f, col0, w, inM, inH)
                        P.add("pe", lambda t, b=b, w=w, f=f, d=d, rhs=rhs: t.matmul(
                            psb[b][:, 0:w], lhsT=ring[:, dslots[f], d * 128:(d + 1) * 128], rhs=rhs,
                            start=(f == 0), stop=(f == nf - 1)),
                            reads=[("ring", dslots[f])] + tkeys(inname, kind, f, blocks), writes=[PS(b)])
                    hv = tview(kind, d, col0, w, hMt, hHt)
                    P.add("dve", lambda v, b=b, w=w, hv=hv: v.tensor_tensor(out=hv, in0=psb[b][:, 0:w], in1=hv, op=ALU.add),
                          reads=[PS(b)] + tkeys("h", kind, d, blocks), writes=tkeys("h", kind, d, blocks))
            if after_tile is not None:
                after_tile(len(tiles) - 1)

        def mlp(l, tiles, hMt, hHt, hnMt, hnHt, agMt, agHt, rM, rH, tail=None, skip_stats=False):
            if not skip_stats:
                norm_stats(tiles, hMt, hHt, rM, rH)
            norm_apply(tiles, 7 + l, hMt, hHt, rM, rH, hnMt, hnHt, "hn")

            def evac(f, tl, b):
                (kind, col0, w, blocks) = tl
                i = rr["rl"] % 2
                rr["rl"] += 1
                dst = tview(kind, f, col0, w, agMt, agHt)
                P.add("act", lambda a, b=b, i=i, w=w: a.activation(out=rl[:, i, 0:w], in_=psb[b][:, 0:w], func=AF.Relu),
                      reads=[PS(b)], writes=[("rl", i)])
                P.add("act", lambda a, i=i, w=w, dst=dst: a.activation(out=dst, in_=rl[:, i, 0:w], func=AF.Square),
                      reads=[("rl", i)], writes=tkeys("ag", kind, f, blocks))

            for g in range(DFF // 512):
                us = u_pieces(w_up[l], g * 512)
                ds = d_pieces(w_down[l], g * 512, 4)
                up_like(tiles, us, 4, hnMt, hnHt, evac)
                down_like(tiles, ds, agMt, agHt, "ag", hMt, hHt, after_tile=(tail if g == DFF // 512 - 1 else None))

        kvs = ExitStack()
        E(kvs)
        scr = sb("scr", (128, 3 * EXT), F32, kvs)
        exu = scr[:, 0:EXT]
        exa = scr[:, EXT:2 * EXT]
        exb = scr[:, 2 * EXT:3 * EXT]
        NKT = 128 + NPR + NS
        kT = scr[:, 0:2 * NKT].bitcast(BF16).rearrange("p (f n) -> p f n", f=4)
        v_sb = scr[:, 2 * NKT:2 * NKT + 1280].bitcast(BF16).rearrange("p (b e) -> p b e", e=256)
        EXUK = [("exu", r_) for r_ in ("z", "h", "p", "s", "sp")]
        SCRK = EXUK + [("exa",), ("exb",)]
        sa = ExitStack()
        hH = sb("hH", (128, NCH, NHALO), F32, sa)
        hnH = sb("hnH", (128, NCH, NHALO), BF16, sa)
        agH = sb("agH", (128, 4, NHALO), BF16, sa)
        rstdH = sb("rstdH", (128, NHALO), F32, sa)
        spT = sb("spT", (128, NCH, 60), F32, sa)
        invc = sb("invc", (128, 4, 16), F32, sa)
        t15 = sb("t15", (128, 16), F32, sa)
        usn = sb("usn", (128, 32), F32, sa)
        pq = sb("pq", (128, 2, 512), F32, sa)
        kvs32 = sb("kvs32", (32, 512), F32, sa)

        P.add("sp", lambda s: s.dma_start(out=invc[:], in_=invcd), writes=[("invc",)], dma=True)
        TA = [TILE_H] + TILES_M

        def load_block(src_ap, rows, kind, col0, hkeys_fn):
            for cg in range(4):
                P.add("sp", lambda s, cg=cg: s.dma_start(out=stage[0:rows, cg * 512:(cg + 1) * 512], in_=src_ap[:, cg * 512:(cg + 1) * 512]),
                      writes=[("stage", cg)], dma=True)
            for cg in range(4):
                b = bank("misc")
                for i in range(4):
                    c = cg * 4 + i
                    P.add("pe", lambda t, b=b, i=i, c=c: t.transpose(out=psb[b][:, i * 128:i * 128 + rows],
                                                                  in_=stage[0:rows, c * 128:(c + 1) * 128],
                                                                  identity=ident[0:rows, 0:rows]),
                          reads=[("stage", cg), ("ident",)], writes=[PS(b)])
                src = psb[b][:, :].rearrange("p (i r) -> p i r", r=128)[:, :, 0:rows]
                tgt = (hM if kind == "M" else hH)[:, cg * 4:cg * 4 + 4, col0:col0 + rows]
                wk = []
                for i in range(4):
                    wk += hkeys_fn(cg * 4 + i)
                copy_op(evac_engine(), tgt, src, [PS(b)], wk)

        load_block(xh[0:128, :], 128, "H", 0, lambda c: [("hH", c)])
        load_block(xh[128:160, :], 32, "H", 128, lambda c: [("hH", c)])
        norm_stats([TILE_H], hM, hH, rstdM, rstdH)
        for blk in range(8):
            load_block(xp[blk * 128:(blk + 1) * 128, :], 128, "M", blk * 128, lambda c, blk=blk: [("h", c, blk)])
            if blk == 2:
                norm_stats([TILES_M[0]], hM, hH, rstdM, rstdH)
            if blk == 5:
                norm_stats([TILES_M[1]], hM, hH, rstdM, rstdH)
        load_block(xs[:, :], 32, "M", NPR, lambda c: [("h", c, 8)])
        norm_stats([TILES_M[2]], hM, hH, rstdM, rstdH)
        def load_state(l):
            for cg in range(4):
                P.add("sp", lambda s, l=l, cg=cg: s.dma_start(out=stage[0:60, cg * 512:(cg + 1) * 512], in_=spd[l, :, cg * 512:(cg + 1) * 512]),
                      writes=[("stage", cg)], dma=True)
            for cg in range(4):
                b = bank("misc")
                for i in range(4):
                    c = cg * 4 + i
                    P.add("pe", lambda t, b=b, i=i, c=c: t.transpose(out=psb[b][:, i * 128:i * 128 + 60],
                                                                  in_=stage[0:60, c * 128:(c + 1) * 128],
                                                                  identity=ident[0:60, 0:60]),
                          reads=[("stage", cg), ("ident",)], writes=[PS(b)])
                src = psb[b][:, :].rearrange("p (i r) -> p i r", r=128)[:, :, 0:60]
                copy_op(evac_engine(), spT[:, cg * 4:cg * 4 + 4, :], src, [PS(b)], [("spT",)])
            for s_ in range(4):
                P.add("sp", lambda s, l=l, s_=s_: s.dma_start(out=pool_s[l, s_, 0:7, :], in_=spd[l, s_ * 15 + 8:s_ * 15 + 15, :]),
                      writes=[("o_pool_s0", l, s_)], dma=True)
        for s_ in range(4):
            P.add("sp", lambda s, s_=s_: s.dma_start(out=k_s[s_, 0:120, :], in_=ck[s_, 8:128, :]), writes=[("o_ks0", s_)], dma=True)
            P.add("sp", lambda s, s_=s_: s.dma_start(out=v_s[s_, 0:120, :], in_=cv[s_, 8:128, :]), writes=[("o_vs0", s_)], dma=True)

        P.add("dve", lambda v: v.memset(exu[:, 0:15], 0.0), writes=[("exu", "z")])

        def sview(t, a, b_):
            return t[:, 15 + NHALO + NPR:EXT].rearrange("p (s t) -> p s t", t=23)[:, :, a:b_]

        for l in range(2):
            if STAGE < 2 + 2 * l:
                break
            load_state(l)
            TAl = [("H", 16 * (l + 1), NHALO - 16 * (l + 1), ("h",))] + TILES_M
            allb = tuple(range(9))
            pool_slots = [u_pieces(w_pool[l, g], 0) for g in range(4)]
            for c in range(NCH):
                g = c // 4
                wnd = POOL_W[g]
                hk = [("h", c, b) for b in allb] + [("hH", c)]
                rk = [("rstd", 0, b) for b in allb] + [("rstdH", 0)]
                ga = gv[:, l, c:c + 1]
                P.add("dve", lambda v, c=c, ga=ga: v.scalar_tensor_tensor(out=exu[:, 15:15 + NHALO], in0=hH[:, c, :], scalar=ga,
                                                                      in1=rstdH[:, :], op0=ALU.mult, op1=ALU.mult),
                      reads=hk + rk + [("gv",)], writes=[("exu", "h")])
                P.add("dve", lambda v, c=c, ga=ga: v.scalar_tensor_tensor(out=exu[:, 15 + NHALO:15 + NHALO + NPR], in0=hM[:, c, 0:NPR],
                                                                      scalar=ga, in1=rstdM[:, 0:NPR], op0=ALU.mult, op1=ALU.mult),
                      reads=hk + rk, writes=[("exu", "p")])
                P.add("dve", lambda v, c=c, ga=ga: v.scalar_tensor_tensor(
                    out=sview(exu, 15, 23), in0=hM[:, c, NPR:NM].rearrange("p (s t) -> p s t", t=8), scalar=ga,
                    in1=rstdM[:, NPR:NM].rearrange("p (s t) -> p s t", t=8), op0=ALU.mult, op1=ALU.mult),
                    reads=hk + rk, writes=[("exu", "s")])
                P.add("act", lambda a, c=c, l=l: a.activation(out=sview(exu, 0, 15), in_=spT[:, c, :].rearrange("p (s t) -> p s t", t=15),
                                                            func=AF.Copy),
                      reads=[("spT",)], writes=[("exu", "sp")])
                P.add("act", lambda a: a.activation(out=usn[:, :].rearrange("p (s t) -> p s t", t=8), in_=sview(exu, 15, 23), func=AF.Copy),
                      reads=[("exu", "s")], writes=[("usn",)])
                i4 = c % 4
                if i4 == 0:
                    bp = bank("misc")
                P.add("pe", lambda t, bp=bp, i4=i4: t.transpose(out=psb[bp][0:15, i4 * 128:(i4 + 1) * 128],
                                                               in_=exu[:, 15 + NHALO + NPR - 15:15 + NHALO + NPR], identity=ident[:, :]),
                      reads=[("exu", "p"), ("ident",)], writes=[PS(bp)])
                if i4 == 0:
                    bs = bank("misc")
                P.add("pe", lambda t, bs=bs, i4=i4: t.transpose(out=psb[bs][0:32, i4 * 128:(i4 + 1) * 128], in_=usn[:, :], identity=ident[:, :]),
                      reads=[("usn",), ("ident",)], writes=[PS(bs)])
                if i4 == 3:
                    cg = c // 4
                    P.add("act", lambda a, bp=bp: a.activation(out=pq[0:15, 0, :], in_=psb[bp][0:15, :], func=AF.Copy),
                          reads=[PS(bp)], writes=[("pq", 0)])
                    P.add("act", lambda a, bs=bs: a.activation(out=pq[0:32, 1, :], in_=psb[bs][0:32, :], func=AF.Copy),
                          reads=[PS(bs)], writes=[("pq", 1)])
                    P.add("sp", lambda s, l=l, cg=cg: s.dma_start(out=pool_p[l, :, cg * 512:(cg + 1) * 512], in_=pq[0:15, 0, :]),
                          reads=[("pq", 0)], writes=[("o_pool_p", l, cg)], dma=True)
                    for s_ in range(4):
                        P.add("sp", lambda s, l=l, cg=cg, s_=s_: s.dma_start(out=pool_s[l, s_, 7:15, cg * 512:(cg + 1) * 512],
                                                                           in_=pq[s_ * 8:(s_ + 1) * 8, 1, :]),
                              reads=[("pq", 1)], writes=[("o_pool_s1", l, cg, s_)], dma=True)
                bufs = [exa, exb]
                cur, curk = exu, EXUK
                sh = 1
                j = 0
                while sh < wnd:
                    nxt = bufs[j % 2]
                    nk = [("exa",)] if j % 2 == 0 else [("exb",)]
                    lo_ = 2 * sh - 1
                    P.add("dve", lambda v, nxt=nxt, cur=cur, sh=sh, lo_=lo_: v.tensor_tensor(
                        out=nxt[:, lo_:EXT], in0=cur[:, lo_:EXT], in1=cur[:, lo_ - sh:EXT - sh], op=ALU.add),
                        reads=curk, writes=nk)
                    cur, curk = nxt, nk
                    sh *= 2
                    j += 1
                S = cur
                iw = 1.0 / wnd
                dkh = [("hnH", c)]
                dkp = [("hn", c, b) for b in range(8)]
                dks = [("hn", c, 8)]
                P.add("dve", lambda v, S=S, c=c, iw=iw: v.scalar_tensor_tensor(out=hnH[:, c, :], in0=S[:, 15:15 + NHALO], scalar=iw,
                                                                            in1=exu[:, 15:15 + NHALO], op0=ALU.mult, op1=ALU.subtract),
                      reads=curk + [("exu", "h")], writes=dkh)
                P.add("dve", lambda v, S=S, c=c, iw=iw: v.scalar_tensor_tensor(out=hnM[:, c, 0:NPR], in0=S[:, 15 + NHALO:15 + NHALO + NPR], scalar=iw,
                                                                            in1=exu[:, 15 + NHALO:15 + NHALO + NPR], op0=ALU.mult, op1=ALU.subtract),
                      reads=curk + [("exu", "p")], writes=dkp)
                P.add("dve", lambda v, S=S, c=c, iw=iw: v.scalar_tensor_tensor(
                    out=hnM[:, c, NPR:NM].rearrange("p (s t) -> p s t", t=8), in0=sview(S, 15, 23), scalar=iw,
                    in1=sview(exu, 15, 23), op0=ALU.mult, op1=ALU.subtract),
                    reads=curk + [("exu", "s")], writes=dks)
                p0 = 15 + NHALO
                P.add("dve", lambda v, S=S, g=g: v.tensor_tensor(out=t15[:, 0:15], in0=S[:, p0:p0 + 15], in1=invc[:, g, 0:15], op=ALU.mult),
                      reads=curk + [("invc",)], writes=[("t15",)])
                P.add("dve", lambda v, c=c: v.tensor_tensor(out=hnM[:, c, 0:15], in0=t15[:, 0:15], in1=exu[:, p0:p0 + 15], op=ALU.subtract),
                      reads=[("t15",), ("exu", "p")], writes=[("hn", c, 0)])
            for (kind, col0, w, blocks) in TAl:
                for g in range(4):
                    slots = pool_slots[g]
                    for e in range(4):
                        oc = 4 * g + e
                        b = bank("up")
                        for kc in range(4):
                            rhs = tview(kind, 4 * g + kc, col0, w, hnM, hnH)
                            P.add("pe", lambda t, b=b, w=w, kc=kc, e=e, rhs=rhs, slots=slots: t.matmul(
                                psb[b][:, 0:w], lhsT=w_u(slots, kc, e * 128), rhs=rhs, start=(kc == 0), stop=(kc == 3)),
                                reads=[("ring", slots[0])] + tkeys("hn", kind, 4 * g + kc, blocks), writes=[PS(b)])
                        hv = tview(kind, oc, col0, w, hM, hH)
                        P.add("dve", lambda v, b=b, w=w, hv=hv, oc=oc, l=l: v.scalar_tensor_tensor(
                            out=hv, in0=psb[b][:, 0:w], scalar=gv[:, 2 + l, oc:oc + 1], in1=hv, op0=ALU.mult, op1=ALU.add),
                            reads=[PS(b), ("gv",)] + tkeys("h", kind, oc, blocks), writes=tkeys("h", kind, oc, blocks))
                norm_stats([(kind, col0, w, blocks)], hM, hH, rstdM, rstdH)
            if STAGE >= 3 + 2 * l:
                mlp(l, TAl, hM, hH, hnM, hnH, agM, agH, rstdM, rstdH,
                    tail=lambda ti: norm_stats([TA[ti]], hM, hH, rstdM, rstdH), skip_stats=True)

        if STAGE >= 6:
            pass
            norm_apply(TA, 4, hM, hH, rstdM, rstdH, hnM, hnH, "hn")
            kd_slots = u_pieces(wkd, 0)
            kv_slots = u_pieces(wkv, 0)
            T_KV = [("H", 32, 128, ("h",))] + TILES_M

            def kt_evac(f, tl, b):
                (kind, col0, w, blocks) = tl
                dcol = 128 + col0 if kind == "M" else 0
                P.add("act", lambda a, b=b, w=w, f=f, dcol=dcol: a.activation(out=kT[:, f, dcol:dcol + w], in_=psb[b][:, 0:w], func=AF.Copy),
                      reads=[PS(b)], writes=[("kT", f, kind, col0)] + SCRK)

            up_like(T_KV, kd_slots, 4, hnM, hnH, kt_evac)
            kv_blocks = [("H", 32, 128, 0)] + [("M", bl * 128, 128, bl + 1) for bl in range(8)] + [("M", NPR, 32, 9)]
            P.add("dve", lambda v: v.memset(v_sb[:, 9, :], 0.0), writes=[("v_sb", 9)] + SCRK)
            for (kind, col0, rows, vi) in kv_blocks:
                b = bank("down")
                for k in range(NCH):
                    lhsT = tview(kind, k, col0, rows, hnM, hnH)
                    rk = [("hnH", k)] if kind == "H" else [("hn", k, min(col0 // 128, 8))]
                    P.add("pe", lambda t, b=b, k=k, rows=rows, lhsT=lhsT: t.matmul(
                        psb[b][0:rows, :], lhsT=lhsT, rhs=ring[:, kv_slots[k // 4], (k % 4) * 512:(k % 4 + 1) * 512],
                        start=(k == 0), stop=(k == NCH - 1)),
                        reads=[("ring", kv_slots[k // 4])] + rk, writes=[PS(b)])
                P.add("act", lambda a, b=b, rows=rows, vi=vi: a.activation(out=v_sb[0:rows, vi, :], in_=psb[b][0:rows, 256:512], func=AF.Copy),
                      reads=[PS(b)], writes=[("v_sb", vi)] + SCRK)
                if vi == 8:
                    P.add("dve", lambda v, b=b: v.tensor_copy(out=kvo[:, :], in_=psb[b][:, :]), reads=[PS(b)], writes=[("kvo",)])
                    P.add("sp", lambda s: s.dma_start(out=k_p[:, :], in_=kvo[:, 0:256]), reads=[("kvo",)], writes=[("o_kp",)], dma=True)
                    P.add("sp", lambda s: s.dma_start(out=v_p[:, :], in_=kvo[:, 256:512]), reads=[("kvo",)], writes=[("o_vp",)], dma=True)
                if vi == 9:
                    P.add("dve", lambda v, b=b: v.tensor_copy(out=kvs32[:, :], in_=psb[b][0:32, :]), reads=[PS(b)], writes=[("kvs32",)])
                    for s_ in range(4):
                        P.add("sp", lambda s, s_=s_: s.dma_start(out=k_s[s_, 120:128, :], in_=kvs32[s_ * 8:(s_ + 1) * 8, 0:256]),
                              reads=[("kvs32",)], writes=[("o_ks1", s_)], dma=True)
                        P.add("sp", lambda s, s_=s_: s.dma_start(out=v_s[s_, 120:128, :], in_=kvs32[s_ * 8:(s_ + 1) * 8, 256:512]),
                              reads=[("kvs32",)], writes=[("o_vs1", s_)], dma=True)

        P.nobar_next = True
        pre_q = {0: u_pieces(w_q[0], 0), 1: u_pieces(w_q[0], 512)}
        P.nobar_next = False
        P.barrier()
        P.emit_block()
        sa.close()

        ogM = sb("ogM", (128, 4, NM), BF16)
        og2raw = sb("og2", (128, 4 * NM), BF16)
        og2 = og2raw[:, :].rearrange("p (f n) -> p f n", f=4)
        ag2 = stage[:, :].bitcast(BF16).rearrange("p (f n) -> p f n", f=4)
        AG2K = [("ag2", f_, b_) for f_ in range(4) for b_ in range(9)]
        OG2K = [("og2", f_, b_) for f_ in range(4) for b_ in range(9)]
        STGK = [("stage", q_) for q_ in range(4)]
        kbT = sb("kbT", (128, 16, 128), BF16)
        vbuf = sb("vbuf", (128, 4, 256), BF16)
        Eb = sb("Eb", (128, 2, 512), BF16)
        Es = sb("Es", (128, 2, 128), BF16)
        maskn = sb("maskn", (128, 512), BF16)
        mask0 = sb("mask0", (128, 512), BF16)
        masksb = sb("masksb", (128, 64), BF16)
        masksn = sb("masksn", (32, 64), BF16)
        esk = sb("esk", (128, 2, 16), F32)
        rec = sb("rec", (128, 2, 128), F32)
        yt = og2raw[:, 0:2048].bitcast(F32).rearrange("p (a i r) -> p a i r", a=2, i=4)

        if STAGE >= 7:
            P.add("dve", lambda v: v.memset(Es[:], 0.0), writes=[("Es", e_, q_) for e_ in range(2) for q_ in range(4)])
            P.add("pool", lambda g: g.dma_start(out=maskn[:], in_=masknd), writes=[("maskn",)], dma=True)
            P.add("pool", lambda g: g.dma_start(out=mask0[:], in_=mask0d), writes=[("mask0",)], dma=True)
            P.add("pool", lambda g: g.dma_start(out=masksb[:], in_=masksbd), writes=[("masksb",)], dma=True)
            P.add("pool", lambda g: g.dma_start(out=masksn[:], in_=masksnd), writes=[("masksn",)], dma=True)
            P.add("pool", lambda g: g.dma_start(out=vbuf[:], in_=cv.rearrange("s k e -> k s e")), writes=[("vbuf",)], dma=True)
            P.add("sp", lambda s: s.dma_start(out=esk[:], in_=skd), writes=[("esk",)], dma=True)
            P.add("act", lambda a: a.activation(out=esk[:], in_=esk[:], func=AF.Exp), reads=[("esk",)], writes=[("esk",)])
            stg = stage[:, 0:2048].rearrange("p (a two e) -> p a two e", two=2, e=64)
            for two in range(2):
                for s_ in range(4):
                    P.add("sp", lambda s, two=two, s_=s_: s.dma_start(out=stg[:, s_ * 4:(s_ + 1) * 4, two, :],
                                                                     in_=ck[s_].rearrange("k (h e) -> k h e", e=64)),
                          writes=[("stage", s_)], dma=True)

        def kbuf_transposes():
            for cg in range(4):
                b = bank("misc")
                for i in range(4):
                    a_ = cg * 4 + i
                    P.add("pe", lambda t, b=b, i=i, a_=a_: t.transpose(out=psb[b][:, i * 128:(i + 1) * 128], in_=stage[:, a_ * 128:(a_ + 1) * 128],
                                                                    identity=ident[:, :]),
                          reads=[("stage", cg), ("ident",)], writes=[PS(b)])
                copy_op(evac_engine(), kbT[:, cg * 4:cg * 4 + 4, :], psb[b][:, :].rearrange("p (i r) -> p i r", r=128), [PS(b)], [("kbT",)])

        TB = [TILES_M[2], TILES_M[0], TILES_M[1]]
        ktkeys = [("kT", f, kind, col0) for f in range(4) for (kind, col0) in (("M", 0), ("M", 384), ("M", 768), ("H", 32))]

        def attention(j, g, qbuf, qname, obuf, oname):
            kk = [("kT", g, kind, col0) for (kind, col0) in (("M", 0), ("M", 384), ("M", 768), ("H", 32))]
            steps = []
            mode = ["s"]

            def A_(*a, **kw):
                steps[-1][mode[0]].append((a, kw))
            for qb in range(8):
                for jj in range(4):
                    ch = 4 * g + jj
                    it = rr.setdefault("att", 0)
                    rr["att"] = it + 1
                    steps.append({"s": [], "r": []})
                    mode[0] = "s"
                    bx, by = (0, 1)
                    bo = (2, 3)[it % 2]
                    ei = it % 2
                    qk = [(qname, jj, qb)]
                    for kb in range(2):
                        for p in range(2):
                            bb = (bx, by)[p]
                            kc0 = (qb + kb) * 128
                            A_("pe", lambda t, bb=bb, kb=kb, p=p, kc0=kc0, jj=jj, qb=qb: t.matmul(
                                psb[bb][:, kb * 128:(kb + 1) * 128], lhsT=kT[p * 64:(p + 1) * 64, g, kc0:kc0 + 128],
                                rhs=qbuf[p * 64:(p + 1) * 64, jj, qb * 128:(qb + 1) * 128], start=True, stop=True),
                                reads=kk + qk, writes=[PS(bb)])
                    for p in range(2):
                        bb = (bx, by)[p]
                        A_("act", lambda a, bb=bb, p=p, ei=ei: a.activation(out=Eb[:, ei, p * 256:(p + 1) * 256], in_=psb[bb][:, 0:256], func=AF.Exp),
                              reads=[PS(bb)], writes=[("Eb", ei, p)])
                    mk = mask0 if qb == 0 else maskn
                    mkk = ("mask0",) if qb == 0 else ("maskn",)
                    A_("dve", lambda v, ei=ei, mk=mk: v.tensor_tensor(out=Eb[:, ei, :], in0=Eb[:, ei, :], in1=mk[:, :], op=ALU.mult),
                          reads=[("Eb", ei, 0), ("Eb", ei, 1), mkk], writes=[("Eb", ei, 0), ("Eb", ei, 1)])
                    mode[0] = "r"
                    for which in range(2):
                        for p in range(2):
                            for kb in range(2):
                                if which == 0:
                                    lhsT = v_sb[:, qb + kb, g * 64:(g + 1) * 64]
                                    rdk = [("v_sb", qb + kb)]
                                else:
                                    lhsT = ones64[:, :]
                                    rdk = [("ones64",)]
                                A_("pe", lambda t, bo=bo, p=p, kb=kb, which=which, lhsT=lhsT, ei=ei: t.matmul(
                                    psb[bo][p * 64:(p + 1) * 64, which * 128:(which + 1) * 128], lhsT=lhsT,
                                    rhs=Eb[:, ei, p * 256 + kb * 128:p * 256 + (kb + 1) * 128], start=(kb == 0), stop=(kb == 1)),
                                    reads=rdk + [("Eb", ei, p)], writes=[PS(bo)])
                    A_("act", lambda a, bo=bo, ei=ei, ch=ch: a.activation(out=rec[:, ei, :], in_=psb[bo][:, 128:256], func=AF.Ln,
                                                                        bias=esk[:, j, ch:ch + 1], scale=1.0),
                          reads=[PS(bo), ("esk",)], writes=[("rec", ei)])
                    A_("act", lambda a, ei=ei: a.activation(out=rec[:, ei, :], in_=rec[:, ei, :], func=AF.Exp, scale=-1.0),
                          reads=[("rec", ei)], writes=[("rec", ei)])
                    A_("dve", lambda v, bo=bo, ei=ei, jj=jj, qb=qb: v.tensor_tensor(out=obuf[:, jj, qb * 128:(qb + 1) * 128], in0=psb[bo][:, 0:128],
                                                                                    in1=rec[:, ei, :], op=ALU.mult),
                          reads=[PS(bo), ("rec", ei)], writes=[(oname, jj, qb)])
            for jj in range(4):
                ch = 4 * g + jj
                it = rr["att"]
                rr["att"] = it + 1
                steps.append({"s": [], "r": []})
                mode[0] = "s"
                bx, by = (0, 1)
                bo = (2, 3)[it % 2]
                ei = it % 2
                qk = [(qname, jj, 8)]
                for s_ in range(4):
                    for p in range(2):
                        bb = (bx, by)[p]
                        A_("pe", lambda t, bb=bb, p=p, s_=s_, jj=jj: t.matmul(
                            psb[bb][:, s_ * 8:(s_ + 1) * 8], lhsT=kbT[p * 64:(p + 1) * 64, s_ * 4 + g, :],
                            rhs=qbuf[p * 64:(p + 1) * 64, jj, NPR + s_ * 8:NPR + (s_ + 1) * 8], start=True, stop=True),
                            reads=[("kbT",)] + qk, writes=[PS(bb)])
                for p in range(2):
                    bb = (bx, by)[p]
                    A_("pe", lambda t, bb=bb, p=p, jj=jj: t.matmul(
                        psb[bb][0:32, 32:64], lhsT=kT[p * 64:(p + 1) * 64, g, 128 + NPR:128 + NM],
                        rhs=qbuf[p * 64:(p + 1) * 64, jj, NPR:NM], start=True, stop=True),
                        reads=kk + qk, writes=[PS(bb)])
                for p in range(2):
                    bb = (bx, by)[p]
                    A_("act", lambda a, bb=bb, p=p, ei=ei: a.activation(out=Es[:, ei, p * 32:(p + 1) * 32], in_=psb[bb][:, 0:32], func=AF.Exp),
                          reads=[PS(bb)], writes=[("Es", ei, p)])
                    A_("act", lambda a, bb=bb, p=p, ei=ei: a.activation(out=Es[0:32, ei, 64 + p * 32:64 + (p + 1) * 32], in_=psb[bb][0:32, 32:64], func=AF.Exp),
                          reads=[PS(bb)], writes=[("Es", ei, 2 + p)])
                ek = [("Es", ei, q_) for q_ in range(4)]
                A_("dve", lambda v, ei=ei: v.tensor_tensor(out=Es[:, ei, 0:64], in0=Es[:, ei, 0:64], in1=masksb[:, :], op=ALU.mult),
                      reads=ek + [("masksb",)], writes=ek)
                A_("dve", lambda v, ei=ei: v.tensor_tensor(out=Es[0:32, ei, 64:128], in0=Es[0:32, ei, 64:128], in1=masksn[:, :], op=ALU.mult),
                      reads=ek + [("masksn",)], writes=ek)
                mode[0] = "r"
                for which in range(2):
                    for p in range(2):
                        oc0 = which * 32
                        l_new = v_sb[:, 9, g * 64:(g + 1) * 64] if which == 0 else ones64[:, :]
                        A_("pe", lambda t, bo=bo, p=p, oc0=oc0, l_new=l_new, ei=ei: t.matmul(
                            psb[bo][p * 64:(p + 1) * 64, oc0:oc0 + 32], lhsT=l_new, rhs=Es[:, ei, 64 + p * 32:64 + (p + 1) * 32],
                            start=True, stop=False),
                            reads=ek + [("v_sb", 9), ("ones64",)], writes=[PS(bo)])
                        for s_ in range(4):
                            l_buf = vbuf[:, s_, g * 64:(g + 1) * 64] if which == 0 else ones64[:, :]
                            A_("pe", lambda t, bo=bo, p=p, oc0=oc0, s_=s_, l_buf=l_buf, ei=ei: t.matmul(
                                psb[bo][p * 64:(p + 1) * 64, oc0 + s_ * 8:oc0 + (s_ + 1) * 8], lhsT=l_buf,
                                rhs=Es[:, ei, p * 32 + s_ * 8:p * 32 + (s_ + 1) * 8], start=False, stop=(s_ == 3)),
                                reads=ek + [("vbuf",), ("ones64",)], writes=[PS(bo)])
                A_("act", lambda a, bo=bo, ei=ei, ch=ch: a.activation(out=rec[:, ei, 0:32], in_=psb[bo][:, 32:64], func=AF.Ln,
                                                                    bias=esk[:, j, ch:ch + 1], scale=1.0),
                      reads=[PS(bo), ("esk",)], writes=[("rec", ei)])
                A_("act", lambda a, ei=ei: a.activation(out=rec[:, ei, 0:32], in_=rec[:, ei, 0:32], func=AF.Exp, scale=-1.0),
                      reads=[("rec", ei)], writes=[("rec", ei)])
                A_("dve", lambda v, bo=bo, ei=ei, jj=jj: v.tensor_tensor(out=obuf[:, jj, NPR:NM], in0=psb[bo][:, 0:32], in1=rec[:, ei, 0:32], op=ALU.mult),
                      reads=[PS(bo), ("rec", ei)], writes=[(oname, jj, 8)])

            chunks = []
            for i_ in range(len(steps)):
                ch_ = []
                if i_ == 0:
                    ch_ += steps[0]["s"]
                if i_ + 1 < len(steps):
                    ch_ += steps[i_ + 1]["s"]
                ch_ += steps[i_]["r"]
                chunks.append(ch_)
            return chunks

        for j in range(2):
            l = 2 + j
            if STAGE < 8 + 2 * j:
                break
            norm_apply(TB, 5 + j, hM, None, rstdM, None, hnM, None, "hn")
            qs, os_ = {}, {}

            def req(tag, g):
                if tag == "q":
                    qs[g] = pre_q[g] if (j == 0 and g < 2) else u_pieces(w_q[j], g * 512)
                else:
                    os_[g] = d_pieces(w_o[j], g * 512, 4)
            qbufs = [(agM, "ag"), (ag2, "ag2")]
            obufs = [(ogM, "og"), (og2, "og2")]

            def capture(fn):
                P.cap = []
                r = fn()
                lst = P.cap
                P.cap = None
                return r, lst

            def replay(lst):
                for (eng, fn, reads, writes, dma) in lst:
                    P.add(eng, fn, reads, writes, dma)

            def q_ops(g, pool):
                qb_, qn_ = qbufs[g % 2]

                def q_evac(f, tl, b):
                    (kind, col0, w, blocks) = tl
                    P.add("act", lambda a, b=b, w=w, f=f, col0=col0: a.activation(out=qb_[:, f, col0:col0 + w], in_=psb[b][:, 0:w],
                                                                               func=AF.Copy, scale=0.125),
                          reads=[PS(b)], writes=tkeys(qn_, kind, f, blocks) + (STGK if qn_ == "ag2" else []))
                return capture(lambda: up_like(TB, qs[g], 4, hnM, None, q_evac, pool=pool))[1]

            def o_ops(g, pool, after_tile=None):
                ob_, on_ = obufs[g % 2]
                return capture(lambda: down_like(TB, os_[g], ob_, None, on_, hM, None, pool=pool, after_tile=after_tile))[1]

            def a_chunks(g):
                qb_, qn_ = qbufs[g % 2]
                ob_, on_ = obufs[g % 2]
                chs = attention(j, g, qb_, qn_, ob_, on_)
                return [[(a[0], a[1], list(kw.get("reads", ())), list(kw.get("writes", ())), kw.get("dma", False)) for (a, kw) in ch_]
                        for ch_ in chs]

            def interleave(chunks, filler):
                n = len(chunks)
                per = -(-len(filler) // n) if filler else 0
                fi = 0
                for ch_ in chunks:
                    replay(ch_)
                    replay(filler[fi:fi + per])
                    fi += per
                replay(filler[fi:])

            req("q", 0)
            req("q", 1)
            replay(q_ops(0, "up"))
            if j == 0:
                kbuf_transposes()
            req("o", 0)
            after = {0: [("q", 2)], 1: [("o", 1), ("q", 3)], 2: [("o", 2), ("o", 3)], 3: []}
            for g in range(4):
                filler = []
                if g >= 1:
                    filler += o_ops(g - 1, "fo")
                if g + 1 < 4:
                    filler += q_ops(g + 1, "fq")
                interleave(a_chunks(g), filler)
                for (tag, gg) in after[g]:
                    req(tag, gg)
            replay(o_ops(3, "down", after_tile=lambda ti: norm_stats([TB[ti]], hM, None, rstdM, None)))
            if STAGE >= 9 + 2 * j:
                mlp(l, TB, hM, None, hnM, None, agM, None, rstdM, None,
                    tail=lambda ti: norm_stats([TB[ti]], hM, None, rstdM, None), skip_stats=True)

        if STAGE >= 12:
            pass
            out_blocks = [(bl * 128, 128, y_p[bl * 128:(bl + 1) * 128, :], bl) for bl in range(8)] + [(NPR, 32, y_s[:, :], 8)]
            for (col0, rows, dst, bl) in out_blocks:
                for cg in range(4):
                    yi = rr.setdefault("yt", 0) % 2
                    rr["yt"] += 1
                    b = bank("misc")
                    for i in range(4):
                        c = cg * 4 + i
                        P.add("dve", lambda v, yi=yi, i=i, c=c, col0=col0, rows=rows: v.scalar_tensor_tensor(
                            out=yt[:, yi, i, 0:rows], in0=hM[:, c, col0:col0 + rows], scalar=gv[:, 11, c:c + 1],
                            in1=rstdM[:, col0:col0 + rows], op0=ALU.mult, op1=ALU.mult),
                            reads=[("h", c, bl), ("rstd", 0, bl), ("gv",)], writes=[("yt", yi, i)] + (OG2K if (bl == 0 and cg < 2) else []))
                        P.add("pe", lambda t, b=b, yi=yi, i=i, rows=rows: t.transpose(out=psb[b][0:rows, i * 128:(i + 1) * 128],
                                                                                   in_=yt[:, yi, i, 0:rows], identity=ident[:, :]),
                              reads=[("yt", yi, i), ("ident",)], writes=[PS(b)])
                    P.add("act", lambda a, b=b, rows=rows, cg=cg: a.activation(out=stage[0:rows, cg * 512:(cg + 1) * 512], in_=psb[b][0:rows, :], func=AF.Copy),
                          reads=[PS(b)], writes=[("stage", cg)] + (AG2K if bl == 0 else []))
                    P.add("sp", lambda s, dst=dst, rows=rows, cg=cg: s.dma_start(out=dst[:, cg * 512:(cg + 1) * 512],
                                                                               in_=stage[0:rows, cg * 512:(cg + 1) * 512]),
                          reads=[("stage", cg)], writes=[("o_y", bl, cg)], dma=True)

        outkeys = [k for k in P.res if isinstance(k[0], str) and k[0].startswith("o_")]
        P.add("sp", None, reads=outkeys)
        P.emit_block()
    return nc


_NC = None


def _get_nc():
    global _NC
    if _NC is None:
        _NC = build_nc()
    return _NC


def _chunked(v):
    return np.ascontiguousarray(np.asarray(v, np.float32).reshape(NCH, 128).T)


def kernel(x_prompt, x_sample, state_pool, cache_k_win, cache_v_win, norm_a, w_pool, pool_scale,
           norm_kv, w_k, w_v, norm_b, w_q, w_o, sinks, norm_mlp, w_up, w_down, norm_f):
    f32 = np.float32
    A = lambda a: np.ascontiguousarray(np.asarray(a, dtype=f32))
    x_prompt = A(x_prompt); x_sample = A(x_sample); state_pool = A(state_pool)
    cache_k_win = A(cache_k_win); cache_v_win = A(cache_v_win)
    w_pool = A(w_pool); w_q = A(w_q); w_o = A(w_o); w_up = A(w_up); w_down = A(w_down)
    w_k = A(w_k); w_v = A(w_v); sinks = A(sinks)
    vecs = [norm_a[0], norm_a[1], pool_scale[0], pool_scale[1], norm_kv, norm_b[0], norm_b[1],
            norm_mlp[0], norm_mlp[1], norm_mlp[2], norm_mlp[3], norm_f]
    gv = np.ascontiguousarray(np.stack([_chunked(v) for v in vecs], axis=1))
    wk4 = w_k.reshape(D, 4, 64)
    wkd = np.ascontiguousarray(np.concatenate([wk4, wk4], axis=2).reshape(D, 512))
    wkv = np.ascontiguousarray(np.concatenate([w_k, w_v], axis=1))
    sk = np.zeros((128, 2, 16), f32)
    for p in range(2):
        sk[p * 64:(p + 1) * 64] = sinks[:, p::2][None, :, :]
    ident = np.eye(128, dtype=f32)
    s_i = np.arange(128)[:, None]
    q_i = np.arange(128)[None, :]
    m_prev = (s_i > q_i).astype(f32)
    m_own = (s_i <= q_i).astype(f32)
    one_p = np.concatenate([m_prev, m_own], axis=1)
    maskn = np.concatenate([one_p, one_p], axis=1)
    one_p0 = np.concatenate([np.zeros_like(m_prev), m_own], axis=1)
    mask0_first = np.concatenate([one_p0, one_p0], axis=1)
    t_i = np.tile(np.arange(8), 4)[None, :]
    msb1 = (np.arange(128)[:, None] > t_i).astype(f32)
    masksb = np.concatenate([msb1, msb1], axis=1)
    ks = np.arange(32) // 8; kt = np.arange(32) % 8
    msn1 = ((ks[:, None] == ks[None, :]) & (kt[:, None] <= kt[None, :])).astype(f32)
    masksn = np.concatenate([msn1, msn1], axis=1)
    invc_first = np.zeros((128, 4, 16), f32)
    invc_rest = np.zeros((128, 4, 16), f32)
    for g, w in enumerate(POOL_W):
        invc_first[:, g, :] = 1.0 / np.minimum(np.arange(16) + 1, w)
        invc_rest[:, g, :] = 1.0 / w

    in_maps = []
    for c in range(8):
        b, half = c // 2, c % 2
        s0 = half * NPR
        if half == 0:
            xh = np.zeros((NHALO, D), f32)
        else:
            xh = np.ascontiguousarray(x_prompt[b, s0 - NHALO:s0])
        in_maps.append({
            "xh": xh,
            "xp": np.ascontiguousarray(x_prompt[b, s0:s0 + NPR]),
            "xs": np.ascontiguousarray(x_sample[4 * c:4 * c + 4].reshape(NS, D)),
            "sp": np.ascontiguousarray(state_pool[:, 4 * c:4 * c + 4].reshape(2, 60, D)),
            "ck": np.ascontiguousarray(cache_k_win[4 * c:4 * c + 4].reshape(4, 128, 256)),
            "cv": np.ascontiguousarray(cache_v_win[4 * c:4 * c + 4].reshape(4, 128, 256)),
            "gv": gv, "sk": sk, "invc": invc_first if half == 0 else invc_rest, "ident": ident,
            "maskn": maskn, "mask0": mask0_first if half == 0 else maskn,
            "masksb": masksb, "masksn": masksn,
            "w_pool": w_pool, "wkd": wkd, "wkv": wkv, "w_q": w_q, "w_o": w_o, "w_up": w_up, "w_down": w_down,
        })
    nc = _get_nc()
    res = run_bass_kernel_spmd(nc, in_maps, core_ids=list(range(8)))
    R = res.results
    y_prompt = np.zeros((4, 2048, D), f32)
    y_sample = np.zeros((32, 8, D), f32)
    pool_p = np.zeros((2, 4, PST, D), f32)
    pool_s = np.zeros((2, 32, PST, D), f32)
    k_p = np.zeros((4, 128, 4, 64), f32); v_p = np.zeros((4, 128, 4, 64), f32)
    k_s = np.zeros((32, 128, 4, 64), f32); v_s = np.zeros((32, 128, 4, 64), f32)
    for c in range(8):
        b, half = c // 2, c % 2
        r = R[c]
        y_prompt[b, half * NPR:(half + 1) * NPR] = r["y_p"]
        y_sample[4 * c:4 * c + 4] = r["y_s"].reshape(4, 8, D)
        pool_s[:, 4 * c:4 * c + 4] = r["pool_s"]
        k_s[4 * c:4 * c + 4] = r["k_s"].reshape(4, 128, 4, 64)
        v_s[4 * c:4 * c + 4] = r["v_s"].reshape(4, 128, 4, 64)
        if half == 1:
            pool_p[:, b] = r["pool_p"]
            k_p[b] = r["k_p"].reshape(128, 4, 64)
            v_p[b] = r["v_p"].reshape(128, 4, 64)
    return (y_prompt, y_sample, pool_p, pool_s, k_p, v_p, k_s, v_s)
```

```python
import numpy as np
from contextlib import ExitStack
import concourse.bass as bass
import concourse.mybir as mybir
from concourse.bass_utils import run_bass_kernel_spmd

F32 = mybir.dt.float32
BF16 = mybir.dt.bfloat16
AF = mybir.ActivationFunctionType
ALU = mybir.AluOpType

D = 2048
NCH = 16
DFF = 8192
NPR = 1024
NHALO = 160
NS = 32
NM = NPR + NS
PST = 15
EPS = 1e-5
NSLOT = 8
EXT = 15 + NHALO + NPR + 4 * 23
POOL_W = (2, 4, 8, 16)

TILES_M = [("M", 0, 384, (0, 1, 2)), ("M", 384, 384, (3, 4, 5)), ("M", 768, 288, (6, 7, 8))]
TILE_H = ("H", 0, NHALO, ("h",))


class Prog:
    ENGS = ("pe", "act", "dve", "pool", "sp")
    EPOCH = 4096
    NEP = 8
    KDMA = 12

    def __init__(self, nc, es):
        self.nc = nc
        self.ops = []
        self.res = {}
        self.emitted = 0
        self.cnt = {e: 0 for e in self.ENGS}
        self.sems = {e: [es.enter_context(nc.semaphore(f"s_{e}_{i}")) for i in range(self.NEP)]
                     for e in ("pe", "act", "dve")}
        self.sems["pool"] = [es.enter_context(nc.semaphore(f"s_pool_{i}")) for i in range(2)]
        self.sems["sp"] = [es.enter_context(nc.semaphore(f"s_sp_{i}")) for i in range(2)]
        self.dsems = {q: [es.enter_context(nc.semaphore(f"d_{q}_{i}")) for i in range(self.KDMA)]
                      for q in ("pool", "sp")}
        self.duse = {q: [0] * self.KDMA for q in ("pool", "sp")}
        self.dn = {q: 0 for q in ("pool", "sp")}
        self.waited = {e: {} for e in self.ENGS}
        self.floor = 0
        self.barrier_tokens = []
        self.need_bw = {e: False for e in self.ENGS}

    cap = None

    nobar_next = False

    def add(self, eng, fn, reads=(), writes=(), dma=False):
        if self.cap is not None:
            self.cap.append((eng, fn, list(reads), list(writes), dma))
            return None
        oid = len(self.ops)
        deps = {}
        psr = [k for k in reads if k[0] == "ps"]
        if psr:
            writes = list(writes) + [k for k in psr if k not in writes]

        def need(p, raw):
            po = self.ops[p]
            if po["dma"]:
                deps[("d", p)] = p
                return
            if p < self.floor:
                return
            if (not dma) and po["eng"] == eng and eng == "pe":
                return
            k = ("c", po["eng"])
            if deps.get(k, -1) < p:
                deps[k] = p

        for k in reads:
            r = self.res.get(k)
            if r is not None and r[0] is not None:
                need(r[0], True)
        for k in writes:
            r = self.res.get(k)
            if r is not None:
                if r[0] is not None:
                    need(r[0], False)
                for p in r[1].values():
                    need(p, False)
        ek = ("d", oid) if dma else eng
        for k in reads:
            r = self.res.setdefault(k, [None, {}])
            r[1][ek] = oid
        for k in writes:
            self.res[k] = [oid, {}]
        for p in deps.values():
            self.ops[p]["inc"] = True
        self.ops.append(dict(id=oid, eng=eng, fn=fn, dma=dma, deps=list(deps.values()), inc=False, tok=None,
                             nobar=(dma and self.nobar_next)))
        return oid

    def barrier(self):
        last = {}
        for op in self.ops:
            if not op["dma"] and op["fn"] is not None:
                last[op["eng"]] = op["id"]
        for p in last.values():
            self.ops[p]["inc"] = True
        self._pending_barrier = list(last.values())

    def _assign(self, lo, hi):
        for op in self.ops[lo:hi]:
            if op["dma"]:
                q = op["eng"]
                i = self.dn[q] % self.KDMA
                u = self.duse[q][i]
                op["tok"] = (("d", q, i), 16 * (u + 1))
                op["prev"] = (("d", q, i), 16 * u) if u > 0 else None
                self.duse[q][i] += 1
                self.dn[q] += 1
            elif op["inc"]:
                e = op["eng"]
                self.cnt[e] += 1
                ep = (self.cnt[e] - 1) // self.EPOCH
                assert ep < len(self.sems[e]), ("too many incs", e)
                op["tok"] = (("c", e, ep), (self.cnt[e] - 1) % self.EPOCH + 1)

    def _sem(self, key):
        if key[0] == "d":
            return self.dsems[key[1]][key[2]]
        return self.sems[key[1]][key[2]]

    def emit_block(self):
        lo, hi = self.emitted, len(self.ops)
        self._assign(lo, hi)
        nc = self.nc
        bw = getattr(self, "_bw_tokens", [])

        def run(ename):
            def body(engine):
                w = self.waited[ename]

                def wait(tok):
                    key, val = tok
                    if w.get(key, 0) >= val:
                        return
                    engine.wait_ge(self._sem(key), val)
                    w[key] = val

                first = True
                for op in self.ops[lo:hi]:
                    if op["eng"] != ename:
                        continue
                    if first:
                        for t in bw:
                            wait(t)
                        first = False
                    for p in op["deps"]:
                        wait(self.ops[p]["tok"])
                    if op["dma"] and op["prev"] is not None:
                        wait(op["prev"])
                    if op["fn"] is None:
                        continue
                    inst = op["fn"](engine)
                    if op["tok"] is not None:
                        inst.then_inc(self._sem(op["tok"][0]), 16 if op["dma"] else 1)
            return body

        with nc.Block() as block:
            block.tensor(run("pe"))
            block.scalar(run("act"))
            block.vector(run("dve"))
            block.gpsimd(run("pool"))
            block.sync(run("sp"))
        self.emitted = hi
        pend = getattr(self, "_pending_barrier", None)
        if pend:
            toks = [self.ops[p]["tok"] for p in pend]
            dm = {}
            for op in self.ops[lo:hi]:
                if op["dma"] and not op["nobar"]:
                    key, val = op["tok"]
                    dm[key] = max(dm.get(key, 0), val)
            self._bw_tokens = toks + list(dm.items())
            self.floor = hi
            self._pending_barrier = None
        else:
            self._bw_tokens = []


STAGE = 99


def build_nc():
    nc = bass.Bass("TRN2", target_bir_lowering=False)

    def din(name, shape):
        return nc.dram_tensor(name, list(shape), F32, kind="ExternalInput").ap()

    def dout(name, shape):
        return nc.dram_tensor(name, list(shape), F32, kind="ExternalOutput").ap()

    xh = din("xh", (NHALO, D)); xp = din("xp", (NPR, D)); xs = din("xs", (NS, D))
    spd = din("sp", (2, 60, D)); ck = din("ck", (4, 128, 256)); cv = din("cv", (4, 128, 256))
    gvd = din("gv", (128, 12, 16)); skd = din("sk", (128, 2, 16)); invcd = din("invc", (128, 4, 16))
    identd = din("ident", (128, 128))
    masknd = din("maskn", (128, 512)); mask0d = din("mask0", (128, 512))
    masksbd = din("masksb", (128, 64)); masksnd = din("masksn", (32, 64))
    w_pool = din("w_pool", (2, 4, 512, 512)); wkd = din("wkd", (D, 512)); wkv = din("wkv", (D, 512))
    w_q = din("w_q", (2, D, D)); w_o = din("w_o", (2, D, D))
    w_up = din("w_up", (4, D, DFF)); w_down = din("w_down", (4, DFF, D))

    y_p = dout("y_p", (NPR, D)); y_s = dout("y_s", (NS, D))
    pool_p = dout("pool_p", (2, PST, D)); pool_s = dout("pool_s", (2, 4, PST, D))
    k_p = dout("k_p", (128, 256)); v_p = dout("v_p", (128, 256))
    k_s = dout("k_s", (4, 128, 256)); v_s = dout("v_s", (4, 128, 256))

    with ExitStack() as es:
        E = es.enter_context
        P = Prog(nc, es)

        def sb(name, shape, dt, stack=None):
            return (stack or es).enter_context(nc.sbuf_tensor("t_" + name, list(shape), dt))

        hM = sb("hM", (128, NCH, NM), F32)
        hnM = sb("hnM", (128, NCH, NM), BF16)
        agM = sb("agM", (128, 4, NM), BF16)
        ring = sb("ring", (128, NSLOT, 2048), BF16)
        rstdM = sb("rstdM", (128, NM), F32)
        sq = sb("sq", (128, 3, 384), BF16)
        rl = sb("rl", (128, 2, 384), F32)
        stage = sb("stage", (128, 2112), F32)
        gv = sb("gv", (128, 12, 16), F32)
        ident = sb("ident", (128, 128), F32)
        onesn = sb("onesn", (128, 128), BF16)
        ones64 = sb("ones64", (128, 64), BF16)
        kvo = sb("kvo", (128, 512), F32)
        psb = [E(nc.psum_tensor(f"ps{i}", [128, 512], F32)) for i in range(8)]

        rr = {"up": 0, "down": 0, "misc": 0, "fq": 0, "fo": 0, "sq": 0, "rl": 0, "piece": 0, "evac": 0}
        BANKS = {"up": (0, 1, 2), "down": (3, 4, 5), "misc": (6, 7), "fq": (4, 5), "fo": (6, 7)}

        def bank(pool):
            b = BANKS[pool][rr[pool] % len(BANKS[pool])]
            rr[pool] += 1
            return b

        def PS(b):
            return ("ps", b)

        def tview(kind, c, col0, w, M, H):
            t = M if kind == "M" else H
            return t[:, c, col0:col0 + w]

        def tkeys(name, kind, c, blocks):
            if kind == "M":
                return [(name, c, b) for b in blocks]
            return [(name + "H", c)]

        def load_piece(src_ap, view):
            s = rr["piece"] % NSLOT
            rr["piece"] += 1
            if view == "u":
                dst = ring[:, s, :].rearrange("p (k c) -> p k c", c=512)
            else:
                dst = ring[:, s, :]
            P.add("pool", lambda g, dst=dst, src=src_ap: g.dma_start(out=dst, in_=src),
                  writes=[("ring", s)], dma=True)
            return s

        def u_pieces(mat, col0):
            K = mat.shape[0] // 128
            slots = []
            for kk in range(K // 4):
                src = mat[kk * 512:(kk + 1) * 512, col0:col0 + 512].rearrange("(k p) c -> p k c", p=128)
                slots.append(load_piece(src, "u"))
            return slots

        def d_pieces(mat, row0, n):
            return [load_piece(mat[row0 + i * 128: row0 + (i + 1) * 128, :], "d") for i in range(n)]

        def w_u(slots, k, col0, w=128):
            s = slots[k // 4]
            return ring[:, s, (k % 4) * 512 + col0:(k % 4) * 512 + col0 + w]

        def evac_engine():
            rr["evac"] += 1
            return "act" if rr["evac"] % 2 else "dve"

        def copy_op(eng, out, in_, reads, writes):
            if eng == "act":
                P.add("act", lambda a: a.activation(out=out, in_=in_, func=AF.Copy), reads=reads, writes=writes)
            else:
                P.add(eng, lambda v: v.tensor_copy(out=out, in_=in_), reads=reads, writes=writes)

        P.add("sp", lambda s: s.dma_start(out=gv[:], in_=gvd), writes=[("gv",)], dma=True)
        P.add("sp", lambda s: s.dma_start(out=ident[:], in_=identd), writes=[("ident",)], dma=True)
        P.add("dve", lambda v: v.memset(onesn[:], 1.0 / D), writes=[("onesn",)])
        P.add("dve", lambda v: v.memset(ones64[:], 1.0), writes=[("ones64",)])

        def norm_stats(tiles, hMt, hHt, rM, rH):
            for (kind, col0, w, blocks) in tiles:
                b = bank("misc")
                for c in range(NCH):
                    i = rr["sq"] % 3
                    rr["sq"] += 1
                    src = tview(kind, c, col0, w, hMt, hHt)
                    P.add("act", lambda a, i=i, src=src, w=w: a.activation(out=sq[:, i, 0:w], in_=src, func=AF.Square),
                          reads=tkeys("h", kind, c, blocks), writes=[("sq", i)])
                    P.add("pe", lambda t, b=b, i=i, w=w, c=c: t.matmul(psb[b][:, 0:w], lhsT=onesn[:], rhs=sq[:, i, 0:w],
                                                                    start=(c == 0), stop=(c == NCH - 1)),
                          reads=[("sq", i), ("onesn",)], writes=[PS(b)])
                rt = (rM if kind == "M" else rH)[:, col0:col0 + w]
                rk = tkeys("rstd", kind, 0, blocks)
                P.add("act", lambda a, b=b, rt=rt, w=w: a.activation(out=rt, in_=psb[b][:, 0:w], func=AF.Ln, bias=EPS, scale=1.0),
                      reads=[PS(b)], writes=rk)
                P.add("act", lambda a, rt=rt: a.activation(out=rt, in_=rt, func=AF.Exp, scale=-0.5), reads=rk, writes=rk)

        def norm_apply(tiles, gi, hMt, hHt, rM, rH, oM, oH, oname):
            for (kind, col0, w, blocks) in tiles:
                rt = (rM if kind == "M" else rH)[:, col0:col0 + w]
                for c in range(NCH):
                    src = tview(kind, c, col0, w, hMt, hHt)
                    dst = tview(kind, c, col0, w, oM, oH)
                    P.add("dve", lambda v, dst=dst, src=src, c=c, rt=rt: v.scalar_tensor_tensor(
                        out=dst, in0=src, scalar=gv[:, gi, c:c + 1], in1=rt, op0=ALU.mult, op1=ALU.mult),
                        reads=tkeys("h", kind, c, blocks) + tkeys("rstd", kind, 0, blocks) + [("gv",)],
                        writes=tkeys(oname, kind, c, blocks))

        def up_like(tiles, slots, nout, inM, inH, evac, pool="up"):
            for tl in tiles:
                for f in range(nout):
                    (kind, col0, w, blocks) = tl
                    b = bank(pool)
                    for k in range(NCH):
                        rhs = tview(kind, k, col0, w, inM, inH)
                        P.add("pe", lambda t, b=b, w=w, k=k, f=f, rhs=rhs: t.matmul(
                            psb[b][:, 0:w], lhsT=w_u(slots, k, f * 128), rhs=rhs, start=(k == 0), stop=(k == NCH - 1)),
                            reads=[("ring", slots[k // 4])] + tkeys("hn", kind, k, blocks), writes=[PS(b)])
                    evac(f, tl, b)

        def down_like(tiles, dslots, inM, inH, inname, hMt, hHt, pool="down", after_tile=None):
            nf = len(dslots)
            for ti_, (kind, col0, w, blocks) in enumerate(tiles):
                if after_tile is not None and ti_ > 0:
                    after_tile(ti_ - 1)
                for d in range(NCH):
                    b = bank(pool)
                    for f in range(nf):
                        rhs = tview(kind, f, col0, w, inM, inH)
                        P.add("pe", lambda t, b=b, w=w, f=f, d=d, rhs=rhs: t.matmul(
                            psb[b][:, 0:w], lhsT=ring[:, dslots[f], d * 128:(d + 1) * 128], rhs=rhs,
                            start=(f == 0), stop=(f == nf - 1)),
                            reads=[("ring", dslots[f])] + tkeys(inname, kind, f, blocks), writes=[PS(b)])
                    hv = tview(kind, d, col0, w, hMt, hHt)
                    P.add("dve", lambda v, b=b, w=w, hv=hv: v.tensor_tensor(out=hv, in0=psb[b][:, 0:w], in1=hv, op=ALU.add),
                          reads=[PS(b)] + tkeys("h", kind, d, blocks), writes=tkeys("h", kind, d, blocks))
            if after_tile is not None:
                after_tile(len(tiles) - 1)

        def mlp(l, tiles, hMt, hHt, hnMt, hnHt, agMt, agHt, rM, rH, tail=None, skip_stats=False):
            if not skip_stats:
                norm_stats(tiles, hMt, hHt, rM, rH)
            norm_apply(tiles, 7 + l, hMt, hHt, rM, rH, hnMt, hnHt, "hn")

            def evac(f, tl, b):
                (kind, col0, w, blocks) = tl
                i = rr["rl"] % 2
                rr["rl"] += 1
                dst = tview(kind, f, col0, w, agMt, agHt)
                P.add("act", lambda a, b=b, i=i, w=w: a.activation(out=rl[:, i, 0:w], in_=psb[b][:, 0:w], func=AF.Relu),
                      reads=[PS(b)], writes=[("rl", i)])
                P.add("act", lambda a, i=i, w=w, dst=dst: a.activation(out=dst, in_=rl[:, i, 0:w], func=AF.Square),
                      reads=[("rl", i)], writes=tkeys("ag", kind, f, blocks))

            for g in range(DFF // 512):
                us = u_pieces(w_up[l], g * 512)
                ds = d_pieces(w_down[l], g * 512, 4)
                up_like(tiles, us, 4, hnMt, hnHt, evac)
                down_like(tiles, ds, agMt, agHt, "ag", hMt, hHt, after_tile=(tail if g == DFF // 512 - 1 else None))

        kvs = ExitStack()
        E(kvs)
        scr = sb("scr", (128, 3 * EXT), F32, kvs)
        exu = scr[:, 0:EXT]
        exa = scr[:, EXT:2 * EXT]
        exb = scr[:, 2 * EXT:3 * EXT]
        NKT = 128 + NPR + NS
        kT = scr[:, 0:2 * NKT].bitcast(BF16).rearrange("p (f n) -> p f n", f=4)
        v_sb = scr[:, 2 * NKT:2 * NKT + 1280].bitcast(BF16).rearrange("p (b e) -> p b e", e=256)
        EXUK = [("exu", r_) for r_ in ("z", "h", "p", "s", "sp")]
        SCRK = EXUK + [("exa",), ("exb",)]
        sa = ExitStack()
        hH = sb("hH", (128, NCH, NHALO), F32, sa)
        hnH = sb("hnH", (128, NCH, NHALO), BF16, sa)
        agH = sb("agH", (128, 4, NHALO), BF16, sa)
        rstdH = sb("rstdH", (128, NHALO), F32, sa)
        spT = sb("spT", (128, NCH, 60), F32, sa)
        invc = sb("invc", (128, 4, 16), F32, sa)
        t15 = sb("t15", (128, 16), F32, sa)
        usn = sb("usn", (128, 32), F32, sa)
        pq = sb("pq", (128, 2, 512), F32, sa)
        kvs32 = sb("kvs32", (32, 512), F32, sa)

        P.add("sp", lambda s: s.dma_start(out=invc[:], in_=invcd), writes=[("invc",)], dma=True)
        TA = [TILE_H] + TILES_M

        def load_block(src_ap, rows, kind, col0, hkeys_fn):
            for cg in range(4):
                P.add("sp", lambda s, cg=cg: s.dma_start(out=stage[0:rows, cg * 512:(cg + 1) * 512], in_=src_ap[:, cg * 512:(cg + 1) * 512]),
                      writes=[("stage", cg)], dma=True)
            for cg in range(4):
                b = bank("misc")
                for i in range(4):
                    c = cg * 4 + i
                    P.add("pe", lambda t, b=b, i=i, c=c: t.transpose(out=psb[b][:, i * 128:i * 128 + rows],
                                                                  in_=stage[0:rows, c * 128:(c + 1) * 128],
                                                                  identity=ident[0:rows, 0:rows]),
                          reads=[("stage", cg), ("ident",)], writes=[PS(b)])
                src = psb[b][:, :].rearrange("p (i r) -> p i r", r=128)[:, :, 0:rows]
                tgt = (hM if kind == "M" else hH)[:, cg * 4:cg * 4 + 4, col0:col0 + rows]
                wk = []
                for i in range(4):
                    wk += hkeys_fn(cg * 4 + i)
                copy_op(evac_engine(), tgt, src, [PS(b)], wk)

        load_block(xh[0:128, :], 128, "H", 0, lambda c: [("hH", c)])
        load_block(xh[128:160, :], 32, "H", 128, lambda c: [("hH", c)])
        norm_stats([TILE_H], hM, hH, rstdM, rstdH)
        for blk in range(8):
            load_block(xp[blk * 128:(blk + 1) * 128, :], 128, "M", blk * 128, lambda c, blk=blk: [("h", c, blk)])
            if blk == 2:
                norm_stats([TILES_M[0]], hM, hH, rstdM, rstdH)
            if blk == 5:
                norm_stats([TILES_M[1]], hM, hH, rstdM, rstdH)
        load_block(xs[:, :], 32, "M", NPR, lambda c: [("h", c, 8)])
        norm_stats([TILES_M[2]], hM, hH, rstdM, rstdH)
        def load_state(l):
            for cg in range(4):
                P.add("sp", lambda s, l=l, cg=cg: s.dma_start(out=stage[0:60, cg * 512:(cg + 1) * 512], in_=spd[l, :, cg * 512:(cg + 1) * 512]),
                      writes=[("stage", cg)], dma=True)
            for cg in range(4):
                b = bank("misc")
                for i in range(4):
                    c = cg * 4 + i
                    P.add("pe", lambda t, b=b, i=i, c=c: t.transpose(out=psb[b][:, i * 128:i * 128 + 60],
                                                                  in_=stage[0:60, c * 128:(c + 1) * 128],
                                                                  identity=ident[0:60, 0:60]),
                          reads=[("stage", cg), ("ident",)], writes=[PS(b)])
                src = psb[b][:, :].rearrange("p (i r) -> p i r", r=128)[:, :, 0:60]
                copy_op(evac_engine(), spT[:, cg * 4:cg * 4 + 4, :], src, [PS(b)], [("spT",)])
            for s_ in range(4):
                P.add("sp", lambda s, l=l, s_=s_: s.dma_start(out=pool_s[l, s_, 0:7, :], in_=spd[l, s_ * 15 + 8:s_ * 15 + 15, :]),
                      writes=[("o_pool_s0", l, s_)], dma=True)
        for s_ in range(4):
            P.add("sp", lambda s, s_=s_: s.dma_start(out=k_s[s_, 0:120, :], in_=ck[s_, 8:128, :]), writes=[("o_ks0", s_)], dma=True)
            P.add("sp", lambda s, s_=s_: s.dma_start(out=v_s[s_, 0:120, :], in_=cv[s_, 8:128, :]), writes=[("o_vs0", s_)], dma=True)

        P.add("dve", lambda v: v.memset(exu[:, 0:15], 0.0), writes=[("exu", "z")])

        def sview(t, a, b_):
            return t[:, 15 + NHALO + NPR:EXT].rearrange("p (s t) -> p s t", t=23)[:, :, a:b_]

        for l in range(2):
            if STAGE < 2 + 2 * l:
                break
            load_state(l)
            TAl = [("H", 16 * (l + 1), NHALO - 16 * (l + 1), ("h",))] + TILES_M
            allb = tuple(range(9))
            pool_slots = [u_pieces(w_pool[l, g], 0) for g in range(4)]
            for c in range(NCH):
                g = c // 4
                wnd = POOL_W[g]
                hk = [("h", c, b) for b in allb] + [("hH", c)]
                rk = [("rstd", 0, b) for b in allb] + [("rstdH", 0)]
                ga = gv[:, l, c:c + 1]
                P.add("dve", lambda v, c=c, ga=ga: v.scalar_tensor_tensor(out=exu[:, 15:15 + NHALO], in0=hH[:, c, :], scalar=ga,
                                                                      in1=rstdH[:, :], op0=ALU.mult, op1=ALU.mult),
                      reads=hk + rk + [("gv",)], writes=[("exu", "h")])
                P.add("dve", lambda v, c=c, ga=ga: v.scalar_tensor_tensor(out=exu[:, 15 + NHALO:15 + NHALO + NPR], in0=hM[:, c, 0:NPR],
                                                                      scalar=ga, in1=rstdM[:, 0:NPR], op0=ALU.mult, op1=ALU.mult),
                      reads=hk + rk, writes=[("exu", "p")])
                P.add("dve", lambda v, c=c, ga=ga: v.scalar_tensor_tensor(
                    out=sview(exu, 15, 23), in0=hM[:, c, NPR:NM].rearrange("p (s t) -> p s t", t=8), scalar=ga,
                    in1=rstdM[:, NPR:NM].rearrange("p (s t) -> p s t", t=8), op0=ALU.mult, op1=ALU.mult),
                    reads=hk + rk, writes=[("exu", "s")])
                P.add("act", lambda a, c=c, l=l: a.activation(out=sview(exu, 0, 15), in_=spT[:, c, :].rearrange("p (s t) -> p s t", t=15),
                                                            func=AF.Copy),
                      reads=[("spT",)], writes=[("exu", "sp")])
                P.add("act", lambda a: a.activation(out=usn[:, :].rearrange("p (s t) -> p s t", t=8), in_=sview(exu, 15, 23), func=AF.Copy),
                      reads=[("exu", "s")], writes=[("usn",)])
                i4 = c % 4
                if i4 == 0:
                    bp = bank("misc")
                P.add("pe", lambda t, bp=bp, i4=i4: t.transpose(out=psb[bp][0:15, i4 * 128:(i4 + 1) * 128],
                                                               in_=exu[:, 15 + NHALO + NPR - 15:15 + NHALO + NPR], identity=ident[:, :]),
                      reads=[("exu", "p"), ("ident",)], writes=[PS(bp)])
                if i4 == 0:
                    bs = bank("misc")
                P.add("pe", lambda t, bs=bs, i4=i4: t.transpose(out=psb[bs][0:32, i4 * 128:(i4 + 1) * 128], in_=usn[:, :], identity=ident[:, :]),
                      reads=[("usn",), ("ident",)], writes=[PS(bs)])
                if i4 == 3:
                    cg = c // 4
                    P.add("act", lambda a, bp=bp: a.activation(out=pq[0:15, 0, :], in_=psb[bp][0:15, :], func=AF.Copy),
                          reads=[PS(bp)], writes=[("pq", 0)])
                    P.add("act", lambda a, bs=bs: a.activation(out=pq[0:32, 1, :], in_=psb[bs][0:32, :], func=AF.Copy),
                          reads=[PS(bs)], writes=[("pq", 1)])
                    P.add("sp", lambda s, l=l, cg=cg: s.dma_start(out=pool_p[l, :, cg * 512:(cg + 1) * 512], in_=pq[0:15, 0, :]),
                          reads=[("pq", 0)], writes=[("o_pool_p", l, cg)], dma=True)
                    for s_ in range(4):
                        P.add("sp", lambda s, l=l, cg=cg, s_=s_: s.dma_start(out=pool_s[l, s_, 7:15, cg * 512:(cg + 1) * 512],
                                                                           in_=pq[s_ * 8:(s_ + 1) * 8, 1, :]),
                              reads=[("pq", 1)], writes=[("o_pool_s1", l, cg, s_)], dma=True)
                bufs = [exa, exb]
                cur, curk = exu, EXUK
                sh = 1
                j = 0
                while sh < wnd:
                    nxt = bufs[j % 2]
                    nk = [("exa",)] if j % 2 == 0 else [("exb",)]
                    lo_ = 2 * sh - 1
                    P.add("dve", lambda v, nxt=nxt, cur=cur, sh=sh, lo_=lo_: v.tensor_tensor(
                        out=nxt[:, lo_:EXT], in0=cur[:, lo_:EXT], in1=cur[:, lo_ - sh:EXT - sh], op=ALU.add),
                        reads=curk, writes=nk)
                    cur, curk = nxt, nk
                    sh *= 2
                    j += 1
                S = cur
                iw = 1.0 / wnd
                dkh = [("hnH", c)]
                dkp = [("hn", c, b) for b in range(8)]
                dks = [("hn", c, 8)]
                P.add("dve", lambda v, S=S, c=c, iw=iw: v.scalar_tensor_tensor(out=hnH[:, c, :], in0=S[:, 15:15 + NHALO], scalar=iw,
                                                                            in1=exu[:, 15:15 + NHALO], op0=ALU.mult, op1=ALU.subtract),
                      reads=curk + [("exu", "h")], writes=dkh)
                P.add("dve", lambda v, S=S, c=c, iw=iw: v.scalar_tensor_tensor(out=hnM[:, c, 0:NPR], in0=S[:, 15 + NHALO:15 + NHALO + NPR], scalar=iw,
                                                                            in1=exu[:, 15 + NHALO:15 + NHALO + NPR], op0=ALU.mult, op1=ALU.subtract),
                      reads=curk + [("exu", "p")], writes=dkp)
                P.add("dve", lambda v, S=S, c=c, iw=iw: v.scalar_tensor_tensor(
                    out=hnM[:, c, NPR:NM].rearrange("p (s t) -> p s t", t=8), in0=sview(S, 15, 23), scalar=iw,
                    in1=sview(exu, 15, 23), op0=ALU.mult, op1=ALU.subtract),
                    reads=curk + [("exu", "s")], writes=dks)
                p0 = 15 + NHALO
                P.add("dve", lambda v, S=S, g=g: v.tensor_tensor(out=t15[:, 0:15], in0=S[:, p0:p0 + 15], in1=invc[:, g, 0:15], op=ALU.mult),
                      reads=curk + [("invc",)], writes=[("t15",)])
                P.add("dve", lambda v, c=c: v.tensor_tensor(out=hnM[:, c, 0:15], in0=t15[:, 0:15], in1=exu[:, p0:p0 + 15], op=ALU.subtract),
                      reads=[("t15",), ("exu", "p")], writes=[("hn", c, 0)])
            for g in range(4):
                slots = pool_slots[g]
                for e in range(4):
                    oc = 4 * g + e
                    for (kind, col0, w, blocks) in TAl:
                        b = bank("up")
                        for kc in range(4):
                            rhs = tview(kind, 4 * g + kc, col0, w, hnM, hnH)
                            P.add("pe", lambda t, b=b, w=w, kc=kc, e=e, rhs=rhs, slots=slots: t.matmul(
                                psb[b][:, 0:w], lhsT=w_u(slots, kc, e * 128), rhs=rhs, start=(kc == 0), stop=(kc == 3)),
                                reads=[("ring", slots[0])] + tkeys("hn", kind, 4 * g + kc, blocks), writes=[PS(b)])
                        hv = tview(kind, oc, col0, w, hM, hH)
                        P.add("dve", lambda v, b=b, w=w, hv=hv, oc=oc, l=l: v.scalar_tensor_tensor(
                            out=hv, in0=psb[b][:, 0:w], scalar=gv[:, 2 + l, oc:oc + 1], in1=hv, op0=ALU.mult, op1=ALU.add),
                            reads=[PS(b), ("gv",)] + tkeys("h", kind, oc, blocks), writes=tkeys("h", kind, oc, blocks))
            if STAGE >= 3 + 2 * l:
                mlp(l, TAl, hM, hH, hnM, hnH, agM, agH, rstdM, rstdH,
                    tail=lambda ti: norm_stats([TA[ti]], hM, hH, rstdM, rstdH))

        if STAGE >= 6:
            pass
            norm_apply(TA, 4, hM, hH, rstdM, rstdH, hnM, hnH, "hn")
            kd_slots = u_pieces(wkd, 0)
            kv_slots = u_pieces(wkv, 0)
            T_KV = [("H", 32, 128, ("h",))] + TILES_M

            def kt_evac(f, tl, b):
                (kind, col0, w, blocks) = tl
                dcol = 128 + col0 if kind == "M" else 0
                P.add("act", lambda a, b=b, w=w, f=f, dcol=dcol: a.activation(out=kT[:, f, dcol:dcol + w], in_=psb[b][:, 0:w], func=AF.Copy),
                      reads=[PS(b)], writes=[("kT", f, kind, col0)] + SCRK)

            up_like(T_KV, kd_slots, 4, hnM, hnH, kt_evac)
            kv_blocks = [("H", 32, 128, 0)] + [("M", bl * 128, 128, bl + 1) for bl in range(8)] + [("M", NPR, 32, 9)]
            P.add("dve", lambda v: v.memset(v_sb[:, 9, :], 0.0), writes=[("v_sb", 9)] + SCRK)
            for (kind, col0, rows, vi) in kv_blocks:
                b = bank("down")
                for k in range(NCH):
                    lhsT = tview(kind, k, col0, rows, hnM, hnH)
                    rk = [("hnH", k)] if kind == "H" else [("hn", k, min(col0 // 128, 8))]
                    P.add("pe", lambda t, b=b, k=k, rows=rows, lhsT=lhsT: t.matmul(
                        psb[b][0:rows, :], lhsT=lhsT, rhs=ring[:, kv_slots[k // 4], (k % 4) * 512:(k % 4 + 1) * 512],
                        start=(k == 0), stop=(k == NCH - 1)),
                        reads=[("ring", kv_slots[k // 4])] + rk, writes=[PS(b)])
                P.add("act", lambda a, b=b, rows=rows, vi=vi: a.activation(out=v_sb[0:rows, vi, :], in_=psb[b][0:rows, 256:512], func=AF.Copy),
                      reads=[PS(b)], writes=[("v_sb", vi)] + SCRK)
                if vi == 8:
                    P.add("dve", lambda v, b=b: v.tensor_copy(out=kvo[:, :], in_=psb[b][:, :]), reads=[PS(b)], writes=[("kvo",)])
                    P.add("sp", lambda s: s.dma_start(out=k_p[:, :], in_=kvo[:, 0:256]), reads=[("kvo",)], writes=[("o_kp",)], dma=True)
                    P.add("sp", lambda s: s.dma_start(out=v_p[:, :], in_=kvo[:, 256:512]), reads=[("kvo",)], writes=[("o_vp",)], dma=True)
                if vi == 9:
                    P.add("dve", lambda v, b=b: v.tensor_copy(out=kvs32[:, :], in_=psb[b][0:32, :]), reads=[PS(b)], writes=[("kvs32",)])
                    for s_ in range(4):
                        P.add("sp", lambda s, s_=s_: s.dma_start(out=k_s[s_, 120:128, :], in_=kvs32[s_ * 8:(s_ + 1) * 8, 0:256]),
                              reads=[("kvs32",)], writes=[("o_ks1", s_)], dma=True)
                        P.add("sp", lambda s, s_=s_: s.dma_start(out=v_s[s_, 120:128, :], in_=kvs32[s_ * 8:(s_ + 1) * 8, 256:512]),
                              reads=[("kvs32",)], writes=[("o_vs1", s_)], dma=True)

        P.nobar_next = True
        pre_q = {0: u_pieces(w_q[0], 0), 1: u_pieces(w_q[0], 512)}
        P.nobar_next = False
        P.barrier()
        P.emit_block()
        sa.close()

        ogM = sb("ogM", (128, 4, NM), BF16)
        og2raw = sb("og2", (128, 4 * NM), BF16)
        og2 = og2raw[:, :].rearrange("p (f n) -> p f n", f=4)
        ag2 = stage[:, :].bitcast(BF16).rearrange("p (f n) -> p f n", f=4)
        AG2K = [("ag2", f_, b_) for f_ in range(4) for b_ in range(9)]
        OG2K = [("og2", f_, b_) for f_ in range(4) for b_ in range(9)]
        STGK = [("stage", q_) for q_ in range(4)]
        kbT = sb("kbT", (128, 16, 128), BF16)
        vbuf = sb("vbuf", (128, 4, 256), BF16)
        Eb = sb("Eb", (128, 2, 512), BF16)
        Es = sb("Es", (128, 2, 128), BF16)
        maskn = sb("maskn", (128, 512), BF16)
        mask0 = sb("mask0", (128, 512), BF16)
        masksb = sb("masksb", (128, 64), BF16)
        masksn = sb("masksn", (32, 64), BF16)
        esk = sb("esk", (128, 2, 16), F32)
        rec = sb("rec", (128, 2, 128), F32)
        yt = og2raw[:, 0:2048].bitcast(F32).rearrange("p (a i r) -> p a i r", a=2, i=4)

        if STAGE >= 7:
            P.add("dve", lambda v: v.memset(Es[:], 0.0), writes=[("Es", e_, q_) for e_ in range(2) for q_ in range(4)])
            P.add("pool", lambda g: g.dma_start(out=maskn[:], in_=masknd), writes=[("maskn",)], dma=True)
            P.add("pool", lambda g: g.dma_start(out=mask0[:], in_=mask0d), writes=[("mask0",)], dma=True)
            P.add("pool", lambda g: g.dma_start(out=masksb[:], in_=masksbd), writes=[("masksb",)], dma=True)
            P.add("pool", lambda g: g.dma_start(out=masksn[:], in_=masksnd), writes=[("masksn",)], dma=True)
            P.add("pool", lambda g: g.dma_start(out=vbuf[:], in_=cv.rearrange("s k e -> k s e")), writes=[("vbuf",)], dma=True)
            P.add("sp", lambda s: s.dma_start(out=esk[:], in_=skd), writes=[("esk",)], dma=True)
            P.add("act", lambda a: a.activation(out=esk[:], in_=esk[:], func=AF.Exp), reads=[("esk",)], writes=[("esk",)])
            stg = stage[:, 0:2048].rearrange("p (a two e) -> p a two e", two=2, e=64)
            for two in range(2):
                for s_ in range(4):
                    P.add("sp", lambda s, two=two, s_=s_: s.dma_start(out=stg[:, s_ * 4:(s_ + 1) * 4, two, :],
                                                                     in_=ck[s_].rearrange("k (h e) -> k h e", e=64)),
                          writes=[("stage", s_)], dma=True)

        def kbuf_transposes():
            for cg in range(4):
                b = bank("misc")
                for i in range(4):
                    a_ = cg * 4 + i
                    P.add("pe", lambda t, b=b, i=i, a_=a_: t.transpose(out=psb[b][:, i * 128:(i + 1) * 128], in_=stage[:, a_ * 128:(a_ + 1) * 128],
                                                                    identity=ident[:, :]),
                          reads=[("stage", cg), ("ident",)], writes=[PS(b)])
                copy_op(evac_engine(), kbT[:, cg * 4:cg * 4 + 4, :], psb[b][:, :].rearrange("p (i r) -> p i r", r=128), [PS(b)], [("kbT",)])

        TB = [TILES_M[2], TILES_M[0], TILES_M[1]]
        ktkeys = [("kT", f, kind, col0) for f in range(4) for (kind, col0) in (("M", 0), ("M", 384), ("M", 768), ("H", 32))]

        def attention(j, g, qbuf, qname, obuf, oname):
            kk = [("kT", g, kind, col0) for (kind, col0) in (("M", 0), ("M", 384), ("M", 768), ("H", 32))]
            steps = []
            mode = ["s"]

            def A_(*a, **kw):
                steps[-1][mode[0]].append((a, kw))
            for qb in range(8):
                for jj in range(4):
                    ch = 4 * g + jj
                    it = rr.setdefault("att", 0)
                    rr["att"] = it + 1
                    steps.append({"s": [], "r": []})
                    mode[0] = "s"
                    bx, by = (0, 1)
                    bo = (2, 3)[it % 2]
                    ei = it % 2
                    qk = [(qname, jj, qb)]
                    for kb in range(2):
                        for p in range(2):
                            bb = (bx, by)[p]
                            kc0 = (qb + kb) * 128
                            A_("pe", lambda t, bb=bb, kb=kb, p=p, kc0=kc0, jj=jj, qb=qb: t.matmul(
                                psb[bb][:, kb * 128:(kb + 1) * 128], lhsT=kT[p * 64:(p + 1) * 64, g, kc0:kc0 + 128],
                                rhs=qbuf[p * 64:(p + 1) * 64, jj, qb * 128:(qb + 1) * 128], start=True, stop=True),
                                reads=kk + qk, writes=[PS(bb)])
                    for p in range(2):
                        bb = (bx, by)[p]
                        A_("act", lambda a, bb=bb, p=p, ei=ei: a.activation(out=Eb[:, ei, p * 256:(p + 1) * 256], in_=psb[bb][:, 0:256], func=AF.Exp),
                              reads=[PS(bb)], writes=[("Eb", ei, p)])
                    mk = mask0 if qb == 0 else maskn
                    mkk = ("mask0",) if qb == 0 else ("maskn",)
                    A_("dve", lambda v, ei=ei, mk=mk: v.tensor_tensor(out=Eb[:, ei, :], in0=Eb[:, ei, :], in1=mk[:, :], op=ALU.mult),
                          reads=[("Eb", ei, 0), ("Eb", ei, 1), mkk], writes=[("Eb", ei, 0), ("Eb", ei, 1)])
                    mode[0] = "r"
                    for which in range(2):
                        for p in range(2):
                            for kb in range(2):
                                if which == 0:
                                    lhsT = v_sb[:, qb + kb, g * 64:(g + 1) * 64]
                                    rdk = [("v_sb", qb + kb)]
                                else:
                                    lhsT = ones64[:, :]
                                    rdk = [("ones64",)]
                                A_("pe", lambda t, bo=bo, p=p, kb=kb, which=which, lhsT=lhsT, ei=ei: t.matmul(
                                    psb[bo][p * 64:(p + 1) * 64, which * 128:(which + 1) * 128], lhsT=lhsT,
                                    rhs=Eb[:, ei, p * 256 + kb * 128:p * 256 + (kb + 1) * 128], start=(kb == 0), stop=(kb == 1)),
                                    reads=rdk + [("Eb", ei, p)], writes=[PS(bo)])
                    A_("act", lambda a, bo=bo, ei=ei, ch=ch: a.activation(out=rec[:, ei, :], in_=psb[bo][:, 128:256], func=AF.Ln,
                                                                        bias=esk[:, j, ch:ch + 1], scale=1.0),
                          reads=[PS(bo), ("esk",)], writes=[("rec", ei)])
                    A_("act", lambda a, ei=ei: a.activation(out=rec[:, ei, :], in_=rec[:, ei, :], func=AF.Exp, scale=-1.0),
                          reads=[("rec", ei)], writes=[("rec", ei)])
                    A_("dve", lambda v, bo=bo, ei=ei, jj=jj, qb=qb: v.tensor_tensor(out=obuf[:, jj, qb * 128:(qb + 1) * 128], in0=psb[bo][:, 0:128],
                                                                                    in1=rec[:, ei, :], op=ALU.mult),
                          reads=[PS(bo), ("rec", ei)], writes=[(oname, jj, qb)])
            for jj in range(4):
                ch = 4 * g + jj
                it = rr["att"]
                rr["att"] = it + 1
                steps.append({"s": [], "r": []})
                mode[0] = "s"
                bx, by = (0, 1)
                bo = (2, 3)[it % 2]
                ei = it % 2
                qk = [(qname, jj, 8)]
                for s_ in range(4):
                    for p in range(2):
                        bb = (bx, by)[p]
                        A_("pe", lambda t, bb=bb, p=p, s_=s_, jj=jj: t.matmul(
                            psb[bb][:, s_ * 8:(s_ + 1) * 8], lhsT=kbT[p * 64:(p + 1) * 64, s_ * 4 + g, :],
                            rhs=qbuf[p * 64:(p + 1) * 64, jj, NPR + s_ * 8:NPR + (s_ + 1) * 8], start=True, stop=True),
                            reads=[("kbT",)] + qk, writes=[PS(bb)])
                for p in range(2):
                    bb = (bx, by)[p]
                    A_("pe", lambda t, bb=bb, p=p, jj=jj: t.matmul(
                        psb[bb][0:32, 32:64], lhsT=kT[p * 64:(p + 1) * 64, g, 128 + NPR:128 + NM],
                        rhs=qbuf[p * 64:(p + 1) * 64, jj, NPR:NM], start=True, stop=True),
                        reads=kk + qk, writes=[PS(bb)])
                for p in range(2):
                    bb = (bx, by)[p]
                    A_("act", lambda a, bb=bb, p=p, ei=ei: a.activation(out=Es[:, ei, p * 32:(p + 1) * 32], in_=psb[bb][:, 0:32], func=AF.Exp),
                          reads=[PS(bb)], writes=[("Es", ei, p)])
                    A_("act", lambda a, bb=bb, p=p, ei=ei: a.activation(out=Es[0:32, ei, 64 + p * 32:64 + (p + 1) * 32], in_=psb[bb][0:32, 32:64], func=AF.Exp),
                          reads=[PS(bb)], writes=[("Es", ei, 2 + p)])
                ek = [("Es", ei, q_) for q_ in range(4)]
                A_("dve", lambda v, ei=ei: v.tensor_tensor(out=Es[:, ei, 0:64], in0=Es[:, ei, 0:64], in1=masksb[:, :], op=ALU.mult),
                      reads=ek + [("masksb",)], writes=ek)
                A_("dve", lambda v, ei=ei: v.tensor_tensor(out=Es[0:32, ei, 64:128], in0=Es[0:32, ei, 64:128], in1=masksn[:, :], op=ALU.mult),
                      reads=ek + [("masksn",)], writes=ek)
                mode[0] = "r"
                for which in range(2):
                    for p in range(2):
                        oc0 = which * 32
                        l_new = v_sb[:, 9, g * 64:(g + 1) * 64] if which == 0 else ones64[:, :]
                        A_("pe", lambda t, bo=bo, p=p, oc0=oc0, l_new=l_new, ei=ei: t.matmul(
                            psb[bo][p * 64:(p + 1) * 64, oc0:oc0 + 32], lhsT=l_new, rhs=Es[:, ei, 64 + p * 32:64 + (p + 1) * 32],
                            start=True, stop=False),
                            reads=ek + [("v_sb", 9), ("ones64",)], writes=[PS(bo)])
                        for s_ in range(4):
                            l_buf = vbuf[:, s_, g * 64:(g + 1) * 64] if which == 0 else ones64[:, :]
                            A_("pe", lambda t, bo=bo, p=p, oc0=oc0, s_=s_, l_buf=l_buf, ei=ei: t.matmul(
                                psb[bo][p * 64:(p + 1) * 64, oc0 + s_ * 8:oc0 + (s_ + 1) * 8], lhsT=l_buf,
                                rhs=Es[:, ei, p * 32 + s_ * 8:p * 32 + (s_ + 1) * 8], start=False, stop=(s_ == 3)),
                                reads=ek + [("vbuf",), ("ones64",)], writes=[PS(bo)])
                A_("act", lambda a, bo=bo, ei=ei, ch=ch: a.activation(out=rec[:, ei, 0:32], in_=psb[bo][:, 32:64], func=AF.Ln,
                                                                    bias=esk[:, j, ch:ch + 1], scale=1.0),
                      reads=[PS(bo), ("esk",)], writes=[("rec", ei)])
                A_("act", lambda a, ei=ei: a.activation(out=rec[:, ei, 0:32], in_=rec[:, ei, 0:32], func=AF.Exp, scale=-1.0),
                      reads=[("rec", ei)], writes=[("rec", ei)])
                A_("dve", lambda v, bo=bo, ei=ei, jj=jj: v.tensor_tensor(out=obuf[:, jj, NPR:NM], in0=psb[bo][:, 0:32], in1=rec[:, ei, 0:32], op=ALU.mult),
                      reads=[PS(bo), ("rec", ei)], writes=[(oname, jj, 8)])

            chunks = []
            for i_ in range(len(steps)):
                ch_ = []
                if i_ == 0:
                    ch_ += steps[0]["s"]
                if i_ + 1 < len(steps):
                    ch_ += steps[i_ + 1]["s"]
                ch_ += steps[i_]["r"]
                chunks.append(ch_)
            return chunks

        for j in range(2):
            l = 2 + j
            if STAGE < 8 + 2 * j:
                break
            norm_apply(TB, 5 + j, hM, None, rstdM, None, hnM, None, "hn")
            qs, os_ = {}, {}

            def req(tag, g):
                if tag == "q":
                    qs[g] = pre_q[g] if (j == 0 and g < 2) else u_pieces(w_q[j], g * 512)
                else:
                    os_[g] = d_pieces(w_o[j], g * 512, 4)
            qbufs = [(agM, "ag"), (ag2, "ag2")]
            obufs = [(ogM, "og"), (og2, "og2")]

            def capture(fn):
                P.cap = []
                r = fn()
                lst = P.cap
                P.cap = None
                return r, lst

            def replay(lst):
                for (eng, fn, reads, writes, dma) in lst:
                    P.add(eng, fn, reads, writes, dma)

            def q_ops(g, pool):
                qb_, qn_ = qbufs[g % 2]

                def q_evac(f, tl, b):
                    (kind, col0, w, blocks) = tl
                    P.add("act", lambda a, b=b, w=w, f=f, col0=col0: a.activation(out=qb_[:, f, col0:col0 + w], in_=psb[b][:, 0:w],
                                                                               func=AF.Copy, scale=0.125),
                          reads=[PS(b)], writes=tkeys(qn_, kind, f, blocks) + (STGK if qn_ == "ag2" else []))
                return capture(lambda: up_like(TB, qs[g], 4, hnM, None, q_evac, pool=pool))[1]

            def o_ops(g, pool, after_tile=None):
                ob_, on_ = obufs[g % 2]
                return capture(lambda: down_like(TB, os_[g], ob_, None, on_, hM, None, pool=pool, after_tile=after_tile))[1]

            def a_chunks(g):
                qb_, qn_ = qbufs[g % 2]
                ob_, on_ = obufs[g % 2]
                chs = attention(j, g, qb_, qn_, ob_, on_)
                return [[(a[0], a[1], list(kw.get("reads", ())), list(kw.get("writes", ())), kw.get("dma", False)) for (a, kw) in ch_]
                        for ch_ in chs]

            def interleave(chunks, filler):
                n = len(chunks)
                per = -(-len(filler) // n) if filler else 0
                fi = 0
                for ch_ in chunks:
                    replay(ch_)
                    replay(filler[fi:fi + per])
                    fi += per
                replay(filler[fi:])

            req("q", 0)
            req("q", 1)
            replay(q_ops(0, "up"))
            if j == 0:
                kbuf_transposes()
            req("o", 0)
            after = {0: [("q", 2)], 1: [("o", 1), ("q", 3)], 2: [("o", 2), ("o", 3)], 3: []}
            for g in range(4):
                filler = []
                if g >= 1:
                    filler += o_ops(g - 1, "fo")
                if g + 1 < 4:
                    filler += q_ops(g + 1, "fq")
                interleave(a_chunks(g), filler)
                for (tag, gg) in after[g]:
                    req(tag, gg)
            replay(o_ops(3, "down", after_tile=lambda ti: norm_stats([TB[ti]], hM, None, rstdM, None)))
            if STAGE >= 9 + 2 * j:
                mlp(l, TB, hM, None, hnM, None, agM, None, rstdM, None,
                    tail=lambda ti: norm_stats([TB[ti]], hM, None, rstdM, None), skip_stats=True)

        if STAGE >= 12:
            pass
            out_blocks = [(bl * 128, 128, y_p[bl * 128:(bl + 1) * 128, :], bl) for bl in range(8)] + [(NPR, 32, y_s[:, :], 8)]
            for (col0, rows, dst, bl) in out_blocks:
                for cg in range(4):
                    yi = rr.setdefault("yt", 0) % 2
                    rr["yt"] += 1
                    b = bank("misc")
                    for i in range(4):
                        c = cg * 4 + i
                        P.add("dve", lambda v, yi=yi, i=i, c=c, col0=col0, rows=rows: v.scalar_tensor_tensor(
                            out=yt[:, yi, i, 0:rows], in0=hM[:, c, col0:col0 + rows], scalar=gv[:, 11, c:c + 1],
                            in1=rstdM[:, col0:col0 + rows], op0=ALU.mult, op1=ALU.mult),
                            reads=[("h", c, bl), ("rstd", 0, bl), ("gv",)], writes=[("yt", yi, i)] + (OG2K if (bl == 0 and cg < 2) else []))
                        P.add("pe", lambda t, b=b, yi=yi, i=i, rows=rows: t.transpose(out=psb[b][0:rows, i * 128:(i + 1) * 128],
                                                                                   in_=yt[:, yi, i, 0:rows], identity=ident[:, :]),
                              reads=[("yt", yi, i), ("ident",)], writes=[PS(b)])
                    P.add("act", lambda a, b=b, rows=rows, cg=cg: a.activation(out=stage[0:rows, cg * 512:(cg + 1) * 512], in_=psb[b][0:rows, :], func=AF.Copy),
                          reads=[PS(b)], writes=[("stage", cg)] + (AG2K if bl == 0 else []))
                    P.add("sp", lambda s, dst=dst, rows=rows, cg=cg: s.dma_start(out=dst[:, cg * 512:(cg + 1) * 512],
                                                                               in_=stage[0:rows, cg * 512:(cg + 1) * 512]),
                          reads=[("stage", cg)], writes=[("o_y", bl, cg)], dma=True)

        outkeys = [k for k in P.res if isinstance(k[0], str) and k[0].startswith("o_")]
        P.add("sp", None, reads=outkeys)
        P.emit_block()
    return nc


_NC = None


def _get_nc():
    global _NC
    if _NC is None:
        _NC = build_nc()
    return _NC


def _chunked(v):
    return np.ascontiguousarray(np.asarray(v, np.float32).reshape(NCH, 128).T)


def kernel(x_prompt, x_sample, state_pool, cache_k_win, cache_v_win, norm_a, w_pool, pool_scale,
           norm_kv, w_k, w_v, norm_b, w_q, w_o, sinks, norm_mlp, w_up, w_down, norm_f):
    f32 = np.float32
    A = lambda a: np.ascontiguousarray(np.asarray(a, dtype=f32))
    x_prompt = A(x_prompt); x_sample = A(x_sample); state_pool = A(state_pool)
    cache_k_win = A(cache_k_win); cache_v_win = A(cache_v_win)
    w_pool = A(w_pool); w_q = A(w_q); w_o = A(w_o); w_up = A(w_up); w_down = A(w_down)
    w_k = A(w_k); w_v = A(w_v); sinks = A(sinks)
    vecs = [norm_a[0], norm_a[1], pool_scale[0], pool_scale[1], norm_kv, norm_b[0], norm_b[1],
            norm_mlp[0], norm_mlp[1], norm_mlp[2], norm_mlp[3], norm_f]
    gv = np.ascontiguousarray(np.stack([_chunked(v) for v in vecs], axis=1))
    wk4 = w_k.reshape(D, 4, 64)
    wkd = np.ascontiguousarray(np.concatenate([wk4, wk4], axis=2).reshape(D, 512))
    wkv = np.ascontiguousarray(np.concatenate([w_k, w_v], axis=1))
    sk = np.zeros((128, 2, 16), f32)
    for p in range(2):
        sk[p * 64:(p + 1) * 64] = sinks[:, p::2][None, :, :]
    ident = np.eye(128, dtype=f32)
    s_i = np.arange(128)[:, None]
    q_i = np.arange(128)[None, :]
    m_prev = (s_i > q_i).astype(f32)
    m_own = (s_i <= q_i).astype(f32)
    one_p = np.concatenate([m_prev, m_own], axis=1)
    maskn = np.concatenate([one_p, one_p], axis=1)
    one_p0 = np.concatenate([np.zeros_like(m_prev), m_own], axis=1)
    mask0_first = np.concatenate([one_p0, one_p0], axis=1)
    t_i = np.tile(np.arange(8), 4)[None, :]
    msb1 = (np.arange(128)[:, None] > t_i).astype(f32)
    masksb = np.concatenate([msb1, msb1], axis=1)
    ks = np.arange(32) // 8; kt = np.arange(32) % 8
    msn1 = ((ks[:, None] == ks[None, :]) & (kt[:, None] <= kt[None, :])).astype(f32)
    masksn = np.concatenate([msn1, msn1], axis=1)
    invc_first = np.zeros((128, 4, 16), f32)
    invc_rest = np.zeros((128, 4, 16), f32)
    for g, w in enumerate(POOL_W):
        invc_first[:, g, :] = 1.0 / np.minimum(np.arange(16) + 1, w)
        invc_rest[:, g, :] = 1.0 / w

    in_maps = []
    for c in range(8):
        b, half = c // 2, c % 2
        s0 = half * NPR
        if half == 0:
            xh = np.zeros((NHALO, D), f32)
        else:
            xh = np.ascontiguousarray(x_prompt[b, s0 - NHALO:s0])
        in_maps.append({
            "xh": xh,
            "xp": np.ascontiguousarray(x_prompt[b, s0:s0 + NPR]),
            "xs": np.ascontiguousarray(x_sample[4 * c:4 * c + 4].reshape(NS, D)),
            "sp": np.ascontiguousarray(state_pool[:, 4 * c:4 * c + 4].reshape(2, 60, D)),
            "ck": np.ascontiguousarray(cache_k_win[4 * c:4 * c + 4].reshape(4, 128, 256)),
            "cv": np.ascontiguousarray(cache_v_win[4 * c:4 * c + 4].reshape(4, 128, 256)),
            "gv": gv, "sk": sk, "invc": invc_first if half == 0 else invc_rest, "ident": ident,
            "maskn": maskn, "mask0": mask0_first if half == 0 else maskn,
            "masksb": masksb, "masksn": masksn,
            "w_pool": w_pool, "wkd": wkd, "wkv": wkv, "w_q": w_q, "w_o": w_o, "w_up": w_up, "w_down": w_down,
        })
    nc = _get_nc()
    res = run_bass_kernel_spmd(nc, in_maps, core_ids=list(range(8)))
    R = res.results
    y_prompt = np.zeros((4, 2048, D), f32)
    y_sample = np.zeros((32, 8, D), f32)
    pool_p = np.zeros((2, 4, PST, D), f32)
    pool_s = np.zeros((2, 32, PST, D), f32)
    k_p = np.zeros((4, 128, 4, 64), f32); v_p = np.zeros((4, 128, 4, 64), f32)
    k_s = np.zeros((32, 128, 4, 64), f32); v_s = np.zeros((32, 128, 4, 64), f32)
    for c in range(8):
        b, half = c // 2, c % 2
        r = R[c]
        y_prompt[b, half * NPR:(half + 1) * NPR] = r["y_p"]
        y_sample[4 * c:4 * c + 4] = r["y_s"].reshape(4, 8, D)
        pool_s[:, 4 * c:4 * c + 4] = r["pool_s"]
        k_s[4 * c:4 * c + 4] = r["k_s"].reshape(4, 128, 4, 64)
        v_s[4 * c:4 * c + 4] = r["v_s"].reshape(4, 128, 4, 64)
        if half == 1:
            pool_p[:, b] = r["pool_p"]
            k_p[b] = r["k_p"].reshape(128, 4, 64)
            v_p[b] = r["v_p"].reshape(128, 4, 64)
    return (y_prompt, y_sample, pool_p, pool_s, k_p, v_p, k_s, v_s)
```
